# Optimizing a Trainium2 kernel written in Bass

```python
import math
import numpy as np
import jax
import jax.numpy as jnp
from jax import lax

D_MODEL = 2048
BATCH = 4
SEQ = 2048
DEPTH = 4

N_MIXERS = 4
N_HEADS = 16
HEAD_DIM = 128
WIDTH = N_HEADS * HEAD_DIM
N_KV = 4
HPG = N_HEADS // N_KV
KV_WIDTH = N_KV * HEAD_DIM
QBLOCK = 128
REL_BUCKETS = 32
REL_MAX_DIST = 128
CMP_LEN = 32
CMP_STRIDE = 16
SEL_LEN = 64
SEL_TOPK = 8
SEL_QCHUNK = 64
NSA_WINDOW = 512
SWA_WINDOW = 128
FGATE_BIAS_INIT = 2.0
DN_ALPHA = (2 * DEPTH) ** 0.25
DN_BETA = (8 * DEPTH) ** -0.25
LN_EPS = 1e-5
NEG_INF = -1e30
FORCED_SCORE = 1e4

NSA_SPLITS = (WIDTH,) + (KV_WIDTH,) * 6 + (3 * N_HEADS, WIDTH)
SWA_SPLITS = (WIDTH, KV_WIDTH, KV_WIDTH, WIDTH)
SB_SPLITS = (WIDTH, WIDTH, WIDTH, WIDTH)
FOX_SPLITS = (WIDTH, WIDTH, WIDTH, N_HEADS, WIDTH)

kernel_name = 'hybrid_nsa_swa_stickbreaking_fox'


def split_cols(h, sizes):
    cuts = [int(c) for c in np.cumsum(sizes)[:-1]]
    return jnp.split(h, cuts, axis=-1)


def layer_norm(x, g, b):
    xf = x.astype(jnp.float32)
    mu = jnp.mean(xf, axis=-1, keepdims=True)
    var = jnp.mean(jnp.square(xf - mu), axis=-1, keepdims=True)
    return ((xf - mu) * lax.rsqrt(var + LN_EPS) * g + b).astype(x.dtype)


def rel_bucket(dist):
    dist = jnp.maximum(dist, 0)
    max_exact = REL_BUCKETS // 2
    ratio = jnp.maximum(dist, max_exact).astype(jnp.float32) / max_exact
    large = max_exact + (jnp.log(ratio) / math.log(REL_MAX_DIST / max_exact)
                         * (REL_BUCKETS - max_exact)).astype(jnp.int32)
    return jnp.where(dist < max_exact, dist, jnp.minimum(large, REL_BUCKETS - 1))


def masked_softmax(s, mask, sink=None):
    s = jnp.where(mask, s, NEG_INF)
    m = jnp.max(s, axis=-1, keepdims=True)
    if sink is not None:
        m = jnp.maximum(m, sink)
    p = jnp.where(mask, jnp.exp(s - m), 0.0)
    den = jnp.sum(p, axis=-1, keepdims=True)
    if sink is not None:
        den = den + jnp.exp(sink - m)
    return p / jnp.maximum(den, 1e-30)


def banded_gqa(q, k, v, window, rel_bias, sinks=None):
    b, s_len = q.shape[:2]
    nq = s_len // QBLOCK
    kw = QBLOCK + window
    pad = ((0, 0), (window, 0), (0, 0), (0, 0))
    idx = np.arange(nq)[:, None] * QBLOCK + np.arange(kw)[None, :]
    kb = jnp.pad(k, pad)[:, idx]
    vb = jnp.pad(v, pad)[:, idx]
    qr = q.reshape(b, nq, QBLOCK, N_KV, HPG, HEAD_DIM)
    s = jnp.einsum('bnqgrd,bnkgd->bngrqk', qr, kb).astype(jnp.float32) * HEAD_DIM ** -0.5
    qpos = np.arange(nq)[:, None] * QBLOCK + np.arange(QBLOCK)[None, :]
    kpos = idx - window
    dist = (qpos[:, :, None] - kpos[:, None, :]).astype(np.int32)
    mask = (dist >= 0) & (dist < window) & (kpos[:, None, :] >= 0)
    bias = rel_bias[rel_bucket(jnp.asarray(dist))]
    bias = bias.reshape(nq, QBLOCK, kw, N_KV, HPG).transpose(0, 3, 4, 1, 2)
    s = s + bias[None].astype(jnp.float32)
    mask = jnp.asarray(mask)[None, :, None, None]
    sink = None if sinks is None else sinks.astype(jnp.float32).reshape(1, 1, N_KV, HPG, 1, 1)
    p = masked_softmax(s, mask, sink)
    o = jnp.einsum('bngrqk,bnkgd->bnqgrd', p.astype(v.dtype), vb)
    return o.reshape(b, s_len, N_HEADS, HEAD_DIM)


def nsa_compress(kv, pe, w1, w2):
    n_cmp = (kv.shape[1] - CMP_LEN) // CMP_STRIDE + 1
    idx = np.arange(n_cmp)[:, None] * CMP_STRIDE + np.arange(CMP_LEN)[None, :]
    blocks = kv[:, idx] + pe[None, None, :, None, :]
    hid = jax.nn.silu(jnp.einsum('bnlgd,lde->bnge', blocks, w1))
    return jnp.einsum('bnge,ef->bngf', hid, w2)


def mixer_nsa(h, rel_bias, cmp_pe_k, cmp_w1_k, cmp_w2_k, cmp_pe_v, cmp_w1_v, cmp_w2_v):
    b, s_len = h.shape[:2]
    q, kc, vc, ks, vs, kw, vw, gl, z = split_cols(h, NSA_SPLITS)
    q = q.reshape(b, s_len, N_KV, HPG, HEAD_DIM)

    def kvh(t):
        return t.reshape(b, s_len, N_KV, HEAD_DIM)

    scale = HEAD_DIM ** -0.5
    pos = np.arange(s_len)

    k_cmp = nsa_compress(kvh(kc), cmp_pe_k, cmp_w1_k, cmp_w2_k)
    v_cmp = nsa_compress(kvh(vc), cmp_pe_v, cmp_w1_v, cmp_w2_v)
    n_cmp = k_cmp.shape[1]
    blk_end = np.arange(n_cmp) * CMP_STRIDE + CMP_LEN - 1
    dist_c = (pos[:, None] - blk_end[None, :]).astype(np.int32)
    bias_c = rel_bias[rel_bucket(jnp.asarray(dist_c))].reshape(s_len, n_cmp, N_KV, HPG)
    s_c = (jnp.einsum('bsgrd,bngd->bgrsn', q, k_cmp).astype(jnp.float32) * scale
           + bias_c.transpose(2, 3, 0, 1)[None].astype(jnp.float32))
    p_c = masked_softmax(s_c, jnp.asarray(dist_c >= 0)[None, None, None])
    o_cmp = jnp.einsum('bgrsn,bngd->bsgrd', p_c.astype(v_cmp.dtype), v_cmp)
    o_cmp = o_cmp.reshape(b, s_len, N_HEADS, HEAD_DIM)

    nb = s_len // SEL_LEN
    cstart = np.arange(n_cmp) * CMP_STRIDE
    sstart = np.arange(nb) * SEL_LEN
    inter = np.clip(np.minimum(cstart[:, None] + CMP_LEN, sstart[None, :] + SEL_LEN)
                    - np.maximum(cstart[:, None], sstart[None, :]), 0, None) / CMP_LEN
    imp = jnp.einsum('bgrsn,nj->bgsj', p_c, jnp.asarray(inter, jnp.float32))
    blk = np.arange(nb)
    cur = pos // SEL_LEN
    allowed = blk[None, :] * SEL_LEN <= pos[:, None]
    forced = (blk[None, :] == 0) | (blk[None, :] == cur[:, None]) | (blk[None, :] == cur[:, None] - 1)
    imp = jnp.where(jnp.asarray(forced & allowed), FORCED_SCORE,
                    jnp.where(jnp.asarray(allowed), imp, NEG_INF))
    n_pad = max(nb, SEL_TOPK) - nb
    imp = jnp.pad(imp, ((0, 0), (0, 0), (0, 0), (0, n_pad)), constant_values=NEG_INF)
    top_val, top_idx = lax.top_k(imp, SEL_TOPK)
    blk_ok = top_val > NEG_INF / 2
    top_idx = jnp.minimum(top_idx, nb - 1)

    ks_blocks = kvh(ks).reshape(b, nb, SEL_LEN, N_KV, HEAD_DIM).transpose(0, 3, 1, 2, 4)
    vs_blocks = kvh(vs).reshape(b, nb, SEL_LEN, N_KV, HEAD_DIM).transpose(0, 3, 1, 2, 4)
    nc = s_len // SEL_QCHUNK
    q_chunks = q.reshape(b, nc, SEL_QCHUNK, N_KV, HPG, HEAD_DIM).transpose(1, 0, 2, 3, 4, 5)
    idx_chunks = top_idx.reshape(b, N_KV, nc, SEL_QCHUNK, SEL_TOPK).transpose(2, 0, 1, 3, 4)
    ok_chunks = blk_ok.reshape(b, N_KV, nc, SEL_QCHUNK, SEL_TOPK).transpose(2, 0, 1, 3, 4)
    pos_chunks = jnp.arange(s_len, dtype=jnp.int32).reshape(nc, SEL_QCHUNK)
    bi = jnp.arange(b)[:, None, None, None]
    gi = jnp.arange(N_KV)[None, :, None, None]
    tbl = rel_bias.reshape(REL_BUCKETS, N_KV, HPG).transpose(1, 0, 2)

    def sel_chunk(args):
        qc, ic, okc, tc = args
        kg = ks_blocks[bi, gi, ic].reshape(b, N_KV, SEL_QCHUNK, SEL_TOPK * SEL_LEN, HEAD_DIM)
        vg = vs_blocks[bi, gi, ic].reshape(b, N_KV, SEL_QCHUNK, SEL_TOPK * SEL_LEN, HEAD_DIM)
        kpos = (ic[..., None] * SEL_LEN + jnp.arange(SEL_LEN, dtype=jnp.int32)).reshape(
            b, N_KV, SEL_QCHUNK, SEL_TOPK * SEL_LEN)
        dist = tc[None, None, :, None] - kpos
        mask = jnp.repeat(okc, SEL_LEN, axis=-1) & (dist >= 0)
        bias = tbl[gi, rel_bucket(dist)].transpose(0, 1, 4, 2, 3)
        s = jnp.einsum('bqgrd,bgqkd->bgrqk', qc, kg).astype(jnp.float32) * scale + bias.astype(jnp.float32)
        p = masked_softmax(s, mask[:, :, None])
        return jnp.einsum('bgrqk,bgqkd->bqgrd', p.astype(vg.dtype), vg)

    o_sel = lax.map(sel_chunk, (q_chunks, idx_chunks, ok_chunks, pos_chunks))
    o_sel = o_sel.transpose(1, 0, 2, 3, 4, 5).reshape(b, s_len, N_HEADS, HEAD_DIM)

    o_win = banded_gqa(q.reshape(b, s_len, N_HEADS, HEAD_DIM), kvh(kw), kvh(vw), NSA_WINDOW, rel_bias)

    g = jax.nn.sigmoid(gl.astype(jnp.float32)).reshape(b, s_len, N_HEADS, 3).astype(o_win.dtype)
    o = g[..., 0:1] * o_cmp + g[..., 1:2] * o_sel + g[..., 2:3] * o_win
    return o.reshape(b, s_len, WIDTH), z


def mixer_swa_sinks(h, rel_bias, sinks):
    b, s_len = h.shape[:2]
    q, k, v, z = split_cols(h, SWA_SPLITS)
    o = banded_gqa(q.reshape(b, s_len, N_HEADS, HEAD_DIM),
                   k.reshape(b, s_len, N_KV, HEAD_DIM),
                   v.reshape(b, s_len, N_KV, HEAD_DIM),
                   SWA_WINDOW, rel_bias, sinks)
    return o.reshape(b, s_len, WIDTH), z


def mixer_stick_breaking(h):
    b, s_len = h.shape[:2]
    q, k, v, z = split_cols(h, SB_SPLITS)
    q, k, v = (t.reshape(b, s_len, N_HEADS, HEAD_DIM) for t in (q, k, v))
    scale = HEAD_DIM ** -0.5
    outs = []
    for i in range(s_len // QBLOCK):
        lo, hi = i * QBLOCK, (i + 1) * QBLOCK
        zl = jnp.einsum('bqhd,bkhd->bhqk', q[:, lo:hi], k[:, :hi]).astype(jnp.float32) * scale
        strict = jnp.asarray(np.arange(hi)[None, :] < np.arange(lo, hi)[:, None])
        log_keep = jnp.where(strict, jax.nn.log_sigmoid(-zl), 0.0)
        after = lax.cumsum(log_keep, axis=3, reverse=True) - log_keep
        a = jnp.where(strict, jnp.exp(jax.nn.log_sigmoid(zl) + after), 0.0)
        outs.append(jnp.einsum('bhqk,bkhd->bqhd', a.astype(v.dtype), v[:, :hi]))
    o = jnp.concatenate(outs, axis=1)
    return o.reshape(b, s_len, WIDTH), z


def mixer_forgetting(h, fgate_bias):
    b, s_len = h.shape[:2]
    q, k, v, fl, z = split_cols(h, FOX_SPLITS)
    q, k, v = (t.reshape(b, s_len, N_HEADS, HEAD_DIM) for t in (q, k, v))
    log_f = jax.nn.log_sigmoid((fl + fgate_bias).astype(jnp.float32))
    c = jnp.cumsum(log_f, axis=1).transpose(0, 2, 1)
    scale = HEAD_DIM ** -0.5
    outs = []
    for i in range(s_len // QBLOCK):
        lo, hi = i * QBLOCK, (i + 1) * QBLOCK
        s = (jnp.einsum('bqhd,bkhd->bhqk', q[:, lo:hi], k[:, :hi]).astype(jnp.float32) * scale
             + c[:, :, lo:hi, None] - c[:, :, None, :hi])
        causal = jnp.asarray(np.arange(hi)[None, :] <= np.arange(lo, hi)[:, None])
        p = masked_softmax(s, causal)
        outs.append(jnp.einsum('bhqk,bkhd->bqhd', p.astype(v.dtype), v[:, :hi]))
    o = jnp.concatenate(outs, axis=1)
    return o.reshape(b, s_len, WIDTH), z


def setup_inputs(seed: int = 0) -> dict:
    key = jax.random.key(seed)
    keys = jax.random.split(key, 32)
    count = [0]

    def normal(shape, scale):
        k = keys[count[0]]
        count[0] += 1
        return jax.random.normal(k, shape, jnp.float32) * scale

    w_in_scale = D_MODEL ** -0.5
    w_out_scale = DN_BETA * WIDTH ** -0.5
    inp = {}
    inp['x'] = normal((BATCH, SEQ, D_MODEL), 1.0)
    inp['rel_bias'] = normal((REL_BUCKETS, N_HEADS), 0.5)
    inp['w_in_a'] = normal((D_MODEL, sum(NSA_SPLITS)), w_in_scale)
    inp['w_out_a'] = normal((WIDTH, D_MODEL), w_out_scale)
    inp['ln_g_a'] = 1.0 + normal((D_MODEL,), 0.02)
    inp['ln_b_a'] = normal((D_MODEL,), 0.02)
    inp['cmp_pe_k'] = normal((CMP_LEN, HEAD_DIM), 0.5)
    inp['cmp_w1_k'] = normal((CMP_LEN, HEAD_DIM, HEAD_DIM), (CMP_LEN * HEAD_DIM) ** -0.5)
    inp['cmp_w2_k'] = normal((HEAD_DIM, HEAD_DIM), HEAD_DIM ** -0.5)
    inp['cmp_pe_v'] = normal((CMP_LEN, HEAD_DIM), 0.5)
    inp['cmp_w1_v'] = normal((CMP_LEN, HEAD_DIM, HEAD_DIM), (CMP_LEN * HEAD_DIM) ** -0.5)
    inp['cmp_w2_v'] = normal((HEAD_DIM, HEAD_DIM), HEAD_DIM ** -0.5)
    inp['w_in_b'] = normal((D_MODEL, sum(SWA_SPLITS)), w_in_scale)
    inp['w_out_b'] = normal((WIDTH, D_MODEL), w_out_scale)
    inp['ln_g_b'] = 1.0 + normal((D_MODEL,), 0.02)
    inp['ln_b_b'] = normal((D_MODEL,), 0.02)
    inp['sinks_b'] = normal((N_HEADS,), 0.5)
    inp['w_in_c'] = normal((D_MODEL, sum(SB_SPLITS)), w_in_scale)
    inp['w_out_c'] = normal((WIDTH, D_MODEL), w_out_scale)
    inp['ln_g_c'] = 1.0 + normal((D_MODEL,), 0.02)
    inp['ln_b_c'] = normal((D_MODEL,), 0.02)
    inp['w_in_d'] = normal((D_MODEL, sum(FOX_SPLITS)), w_in_scale)
    inp['w_out_d'] = normal((WIDTH, D_MODEL), w_out_scale)
    inp['ln_g_d'] = 1.0 + normal((D_MODEL,), 0.02)
    inp['ln_b_d'] = normal((D_MODEL,), 0.02)
    inp['fgate_bias_d'] = FGATE_BIAS_INIT + normal((N_HEADS,), 0.5)
    return inp


def reference(x, rel_bias,
              w_in_a, w_out_a, ln_g_a, ln_b_a,
              cmp_pe_k, cmp_w1_k, cmp_w2_k, cmp_pe_v, cmp_w1_v, cmp_w2_v,
              w_in_b, w_out_b, ln_g_b, ln_b_b, sinks_b,
              w_in_c, w_out_c, ln_g_c, ln_b_c,
              w_in_d, w_out_d, ln_g_d, ln_b_d, fgate_bias_d):
    mixers = (
        lambda h: mixer_nsa(h, rel_bias, cmp_pe_k, cmp_w1_k, cmp_w2_k, cmp_pe_v, cmp_w1_v, cmp_w2_v),
        lambda h: mixer_swa_sinks(h, rel_bias, sinks_b),
        mixer_stick_breaking,
        lambda h: mixer_forgetting(h, fgate_bias_d),
    )
    w_in = (w_in_a, w_in_b, w_in_c, w_in_d)
    w_out = (w_out_a, w_out_b, w_out_c, w_out_d)
    ln_g = (ln_g_a, ln_g_b, ln_g_c, ln_g_d)
    ln_b = (ln_b_a, ln_b_b, ln_b_c, ln_b_d)
    for i in range(DEPTH):
        m = i % N_MIXERS
        h = x @ w_in[m]
        o, z = mixers[m](h)
        y = (o * jax.nn.silu(z)) @ w_out[m]
        x = layer_norm(DN_ALPHA * x + y, ln_g[m], ln_b[m])
    return x
```

```python
import numpy as np
from contextlib import ExitStack
import concourse.bass as bass
import concourse.mybir as mybir
from concourse.bass_utils import run_bass_kernel_spmd

F32 = mybir.dt.float32
BF16 = mybir.dt.bfloat16
AF = mybir.ActivationFunctionType
ALU = mybir.AluOpType

SEQ = 2048
DM = 2048
NH = 16
HD = 128
NKV = 4
NTB = SEQ // 128
SCALE = HD ** -0.5
NEG = -30000.0
DN_ALPHA = 8 ** 0.25
LN_EPS = 1e-5
NBT = 896
import os
DBG_HEADS = int(os.environ.get('DBG_HEADS', '16'))
DBG_NOPREP = int(os.environ.get('DBG_NOPREP', '0'))

ENGS = ['pe', 'act', 'dve', 'pool', 'sp']
NSLOT = 8


class Sched:
    def __init__(self, nc, es):
        self.nc = nc
        self.esem = {e: es.enter_context(nc.semaphore("s_" + e)) for e in ENGS}
        self.ecnt = {e: 0 for e in ENGS}
        self.dq = ('sp', 'pool', 'act')
        self.dsem = {e: [es.enter_context(nc.semaphore("d_%s%d" % (e, i))) for i in range(NSLOT)]
                     for e in self.dq}
        self.dcnt = {e: [0] * NSLOT for e in self.dq}
        self.dnext = {e: 0 for e in self.dq}
        self.q = {e: [] for e in ENGS}
        self.waited = {e: {} for e in ENGS}
        self.bufs = {}
        self.ninst = 0

    def _deps(self, reads, writes, pe_chain):
        deps = []
        for k in reads:
            st = self.bufs.get(k)
            if st is not None and st[0] is not None:
                deps.append(st[0])
        for k in writes:
            st = self.bufs.get(k)
            if st is not None:
                if st[0] is not None and not (pe_chain and st[0][0] is self.esem['pe']):
                    deps.append(st[0])
                deps.extend(st[1])
        return deps

    def _filter(self, eng, deps):
        w = self.waited[eng]
        best = {}
        for sem, v in deps:
            key = id(sem)
            if w.get(key, 0) >= v:
                continue
            if key not in best or best[key][1] < v:
                best[key] = (sem, v)
        for key, (sem, v) in best.items():
            w[key] = v
        return list(best.values())

    def _record(self, tok, reads, writes):
        for k in reads:
            st = self.bufs.setdefault(k, [None, []])
            st[1].append(tok)
            if len(st[1]) > 12:
                d = {}
                for s_, v_ in st[1]:
                    if id(s_) not in d or d[id(s_)][1] < v_:
                        d[id(s_)] = (s_, v_)
                st[1] = list(d.values())
        for k in writes:
            self.bufs[k] = [tok, []]

    def op(self, eng, fn, reads=(), writes=(), pe_chain=False):
        deps = self._deps(reads, writes, pe_chain)
        self.ecnt[eng] += 1
        tok = (self.esem[eng], self.ecnt[eng])
        waits = self._filter(eng, deps)
        self.q[eng].append((waits, fn, tok, 1))
        self._record(tok, reads, writes)
        self.ninst += 1
        return tok

    def dma(self, eng, fn, reads=(), writes=()):
        deps = self._deps(reads, writes, False)
        s = self.dnext[eng]
        self.dnext[eng] = (s + 1) % NSLOT
        sem = self.dsem[eng][s]
        if self.dcnt[eng][s] > 0:
            deps.append((sem, 16 * self.dcnt[eng][s]))
        self.dcnt[eng][s] += 1
        tok = (sem, 16 * self.dcnt[eng][s])
        waits = self._filter(eng, deps)
        self.q[eng].append((waits, fn, tok, 16))
        self._record(tok, reads, writes)
        self.ninst += 1
        return tok

    def flush(self):
        nc = self.nc
        tails = {}
        for e in self.dq:
            tl = [(self.dsem[e][s], 16 * self.dcnt[e][s]) for s in range(NSLOT) if self.dcnt[e][s] > 0]
            tails[e] = self._filter(e, tl)
        q = self.q
        self.q = {e: [] for e in ENGS}

        def replay(eobj, lst, tail):
            for waits, fn, tok, inc in lst:
                for sem, v in waits:
                    eobj.wait_ge(sem, v)
                fn(eobj).then_inc(tok[0], inc)
            for sem, v in tail:
                eobj.wait_ge(sem, v)

        with nc.Block() as block:
            @block.tensor
            def _(e):
                replay(e, q['pe'], [])

            @block.scalar
            def _(e):
                replay(e, q['act'], tails['act'])

            @block.vector
            def _(e):
                replay(e, q['dve'], [])

            @block.gpsimd
            def _(e):
                replay(e, q['pool'], tails['pool'])

            @block.sync
            def _(e):
                replay(e, q['sp'], tails['sp'])
        self.bufs = {}


class Ring:
    def __init__(self, name, items):
        self.name = name
        self.items = items
        self.i = 0

    def next(self):
        i = self.i
        self.i = (i + 1) % len(self.items)
        it = self.items[i]
        return it, (self.name, it if isinstance(it, int) else i)


class K:
    def __init__(self, nc, es):
        self.nc = nc
        self.es = es
        self.S = Sched(nc, es)
        self.ps = [es.enter_context(nc.psum_tensor("ps%d" % i, [128, 512], F32)) for i in range(8)]

    def mm(self, out, lhsT, rhs, start, stop, reads, writes):
        return self.S.op('pe', lambda e: e.matmul(out, lhsT=lhsT, rhs=rhs, start=start, stop=stop),
                         reads, writes, pe_chain=True)

    def tr(self, out, in_, ident, reads, writes):
        return self.S.op('pe', lambda e: e.transpose(out, in_, ident), reads, writes, pe_chain=True)

    def act(self, out, in_, func, reads, writes, bias=None, scale=None):
        kw = {}
        if bias is not None:
            kw['bias'] = bias
        if scale is not None:
            kw['scale'] = scale
        return self.S.op('act', lambda e: e.activation(out=out, in_=in_, func=func, **kw), reads, writes)

    def tt(self, eng, out, in0, in1, op, reads, writes):
        return self.S.op(eng, lambda e: e.tensor_tensor(out=out, in0=in0, in1=in1, op=op), reads, writes)

    def ts(self, eng, out, in0, s1, op0, reads, writes, s2=None, op1=None):
        if op1 is None:
            return self.S.op(eng, lambda e: e.tensor_scalar(out=out, in0=in0, scalar1=s1, scalar2=None, op0=op0),
                             reads, writes)
        return self.S.op(eng, lambda e: e.tensor_scalar(out=out, in0=in0, scalar1=s1, scalar2=s2, op0=op0, op1=op1),
                         reads, writes)

    def stt(self, out, in0, scalar, in1, op0, op1, reads, writes):
        return self.S.op('dve', lambda e: e.scalar_tensor_tensor(out=out, in0=in0, scalar=scalar, in1=in1,
                                                                 op0=op0, op1=op1), reads, writes)

    def cp(self, eng, out, in_, reads, writes):
        if eng == 'act':
            return self.S.op('act', lambda e: e.activation(out=out, in_=in_, func=AF.Copy), reads, writes)
        return self.S.op(eng, lambda e: e.tensor_copy(out=out, in_=in_), reads, writes)

    def memset(self, eng, ap, val, writes):
        return self.S.op(eng, lambda e: e.memset(ap, val), (), writes)

    def dma(self, q, out, in_, reads, writes):
        return self.S.dma(q, lambda e: e.dma_start(out=out, in_=in_), reads, writes)


_SBN = [0]


def sbt(nc, ph, name, shape, dt):
    _SBN[0] += 1
    return ph.enter_context(nc.sbuf_tensor("%s_u%d" % (name, _SBN[0]), shape, dt))


def transpose_block(k, C, src, srckey, tb, trring):
    for q4 in range(4):
        bank, bkey = trring.next()
        for u in range(4):
            kc = q4 * 4 + u
            k.tr(k.ps[bank][:, u * 128:(u + 1) * 128], src[:, kc * 128:(kc + 1) * 128], C['ident_f'][:],
                 [srckey, 'const'], [bkey])
        k.cp('act', C['xT'][:, q4 * 4:q4 * 4 + 4, tb * 128:(tb + 1) * 128],
             k.ps[bank][:, :].rearrange("p (u f) -> p u f", u=4), [bkey], [('xT', tb, q4)])


def phase_x0(k, C, x_in, XR):
    nc = k.nc
    with ExitStack() as ph:
        xin = [sbt(nc, ph, "xin%d" % i, [128, DM], F32) for i in range(2)]
        trring = Ring('psb', [6, 7])
        for tb in range(NTB):
            b = tb % 2
            k.dma('sp', xin[b][:], x_in[tb * 128:(tb + 1) * 128, :], [], [('xin', b)])
            k.dma('sp', XR[tb * 128:(tb + 1) * 128, :], xin[b][:], [('xin', b)], [])
            transpose_block(k, C, xin[b], ('xin', b), tb, trring)
        k.S.flush()


def phase_P(k, C, W, groups):
    nc = k.nc
    Wv = W.rearrange("(kc p) c -> p kc c", p=128)
    xT = C['xT']
    with ExitStack() as ph:
        wst = [sbt(nc, ph, "wst%d" % i, [128, 8, 512], F32) for i in range(2)]
        wbf = [sbt(nc, ph, "wbf%d" % i, [128, 16, 512], BF16) for i in range(2)]
        sgb = Ring('sgb', [sbt(nc, ph, "sgb%d" % i, [128, SEQ], BF16) for i in range(3)])
        sgf = Ring('sgf', [sbt(nc, ph, "sgf%d" % i, [128, SEQ], F32) for i in range(2)])
        sgv = sbt(nc, ph, "sgv", [128, 4, NTB, 128], BF16)
        sfl = sbt(nc, ph, "sfl", [128, NTB, 16], F32)
        pring = Ring('psb', [0, 1, 2, 3])
        ev = [0]
        wsti = [0]

        def evac(out, in_, reads, writes):
            ev[0] += 1
            k.cp('act' if ev[0] % 2 else 'dve', out, in_, reads, writes)

        def load_group(gi):
            kind, c0, n, dst = groups[gi]
            wb = gi % 2
            for half in range(2):
                ws = wsti[0] % 2
                wsti[0] += 1
                k.dma('sp', wst[ws][:, :, :n], Wv[:, half * 8:(half + 1) * 8, c0:c0 + n], [], [('wst', ws)])
                k.cp('pool', wbf[wb][:, half * 8:(half + 1) * 8, :n], wst[ws][:, :, :n], [('wst', ws)],
                     [('wbf', wb, half)])

        load_group(0)
        for gi, (kind, c0, n, dst) in enumerate(groups):
            wb = gi % 2
            if gi + 1 < len(groups):
                load_group(gi + 1)
            wkeys = [('wbf', wb, 0), ('wbf', wb, 1)]
            if kind in ('FMB', 'FMF'):
                nct = (n + 127) // 128
                for ct in range(nct):
                    m_ = min(128, n - ct * 128)
                    stg, skey = (sgb if kind == 'FMB' else sgf).next()
                    for tc in range(4):
                        bank, bkey = pring.next()
                        for kc in range(16):
                            k.mm(k.ps[bank][:m_, :], wbf[wb][:, kc, ct * 128:ct * 128 + m_],
                                 xT[:, kc, tc * 512:(tc + 1) * 512], kc == 0, kc == 15,
                                 [wkeys[kc // 8]], [bkey])
                        evac(stg[:m_, tc * 512:(tc + 1) * 512], k.ps[bank][:m_, :], [bkey], [skey])
                    k.dma('sp', dst[ct], stg[:m_, :], [skey], [])
            elif kind == 'TMV':
                ns = n // 128
                for tb in range(NTB):
                    bank, bkey = pring.next()
                    for kc in range(16):
                        k.mm(k.ps[bank][:, :n], xT[:, kc, tb * 128:(tb + 1) * 128], wbf[wb][:, kc, :n],
                             kc == 0, kc == 15, [wkeys[kc // 8]], [bkey])
                    evac(sgv[:, 0:ns, tb, :], k.ps[bank][:, :n].rearrange("p (s d) -> p s d", d=128), [bkey],
                         ['sgv'])
                for si in range(ns):
                    k.dma('sp', dst[si], sgv[:, si, :, :], ['sgv'], [])
            else:
                for tb in range(NTB):
                    bank, bkey = pring.next()
                    for kc in range(16):
                        k.mm(k.ps[bank][:, :n], xT[:, kc, tb * 128:(tb + 1) * 128], wbf[wb][:, kc, :n],
                             kc == 0, kc == 15, [wkeys[kc // 8]], [bkey])
                    evac(sfl[:, tb, :n], k.ps[bank][:, :n], [bkey], ['sfl'])
                k.dma('sp', dst, sfl[:, :, :n], ['sfl'], [])
        k.S.flush()


def phase_B(k, C, Wout, lng_b, lnb_b, G, XR, xout, last):
    nc = k.nc
    Wv = Wout.rearrange("(h p) c -> p h c", p=128)
    with ExitStack() as ph:
        wo = sbt(nc, ph, "wo", [128, NH, DM], BF16)
        wst = [sbt(nc, ph, "wost%d" % i, [128, DM], F32) for i in range(2)]
        gbc = sbt(nc, ph, "gbc", [128, DM], F32)
        bbc = sbt(nc, ph, "bbc", [128, DM], F32)
        gt = [sbt(nc, ph, "gt%d" % i, [128, NH, 128], BF16) for i in range(2)]
        xr = [sbt(nc, ph, "xr%d" % i, [128, DM], F32) for i in range(2)]
        st6 = sbt(nc, ph, "st6", [128, 4, 6], F32)
        mv = sbt(nc, ph, "mv", [128, 2], F32)
        rstd = sbt(nc, ph, "rstd", [128, 1], F32)
        k.dma('sp', gbc[:], lng_b, [], ['gbc'])
        k.dma('sp', bbc[:], lnb_b, [], ['bbc'])
        Gv = G.rearrange("h p s -> p h s")

        def load_tb(tb):
            b = tb % 2
            k.dma('sp', gt[b][:], Gv[:, :, tb * 128:(tb + 1) * 128], [], [('gt', b)])
            k.dma('sp', xr[b][:], XR[tb * 128:(tb + 1) * 128, :], [], [('xr', b, cb) for cb in range(4)])

        load_tb(0)
        for i in range(NH):
            ws = i % 2
            k.dma('sp', wst[ws][:], Wv[:, i, :], [], [('wost', ws)])
            k.cp('pool', wo[:, i, :], wst[ws][:], [('wost', ws)], [('wo', i)])
        trring = Ring('psb', [4, 5])
        for tb in range(NTB):
            b = tb % 2
            if tb + 1 < NTB:
                load_tb(tb + 1)
            for cb in range(4):
                for h in range(NH):
                    k.mm(k.ps[cb][:, :], gt[b][:, h, :], wo[:, h, cb * 512:(cb + 1) * 512], h == 0, h == NH - 1,
                         [('gt', b), ('wo', h)], [('psb', cb)])
                xs = xr[b][:, cb * 512:(cb + 1) * 512]
                k.stt(xs, xs, DN_ALPHA, k.ps[cb][:, :], ALU.mult, ALU.add, [('xr', b, cb), ('psb', cb)],
                      [('xr', b, cb)])
                k.S.op('dve', (lambda o, i_: (lambda e: e.bn_stats(out=o, in_=i_)))(st6[:, cb, :], xs),
                       [('xr', b, cb)], [('st6', cb)])
            k.S.op('dve', lambda e: e.bn_aggr(out=mv[:], in_=st6[:].rearrange("p a b -> p (a b)")),
                   [('st6', cb) for cb in range(4)], ['mv'])
            k.act(rstd[:], mv[:, 1:2], AF.Ln, ['mv'], ['rstd'], bias=C['epscol'][:], scale=1.0)
            k.act(rstd[:], rstd[:], AF.Exp, ['rstd'], ['rstd'], scale=-0.5)
            rk = [('xr', b, cb) for cb in range(4)]
            k.ts('dve', xr[b][:], xr[b][:], mv[:, 0:1], ALU.subtract, rk + ['mv', 'rstd'], rk, s2=rstd[:, 0:1],
                 op1=ALU.mult)
            k.tt('pool', xr[b][:], xr[b][:], gbc[:], ALU.mult, rk + ['gbc'], rk)
            k.tt('pool', xr[b][:], xr[b][:], bbc[:], ALU.add, rk + ['bbc'], rk)
            if last:
                k.dma('sp', xout[tb * 128:(tb + 1) * 128, :], xr[b][:], rk, [])
            else:
                k.dma('sp', XR[tb * 128:(tb + 1) * 128, :], xr[b][:], rk, [])
                for q4 in range(4):
                    bank, bkey = trring.next()
                    for u in range(4):
                        kc = q4 * 4 + u
                        k.tr(k.ps[bank][:, u * 128:(u + 1) * 128], xr[b][:, kc * 128:(kc + 1) * 128],
                             C['ident_f'][:], rk + ['const'], [bkey])
                    k.cp('act', C['xT'][:, q4 * 4:q4 * 4 + 4, tb * 128:(tb + 1) * 128],
                         k.ps[bank][:, :].rearrange("p (u f) -> p u f", u=4), [bkey], [('xT', tb, q4)])
        k.S.flush()


class ACtx:
    def __init__(self, k, C, ph, n_pt=4):
        nc = k.nc
        self.k = k
        self.C = C
        self.sc = Ring('psb', [0, 1, 2])
        self.pt = Ring('pt', [sbt(nc, ph, "pt%d" % i, [128, 512], BF16) for i in range(n_pt)])
        self.ob = Ring('ob', [(3, 4), (5, 6)])
        self.ft = Ring('ft', [sbt(nc, ph, "ft%d" % i, [128, 512], F32) for i in range(4)])
        self.gst = Ring('gst', [sbt(nc, ph, "gst%d" % i, [128, SEQ], BF16) for i in range(2)])

    def zero_init(self, banks, cols=512):
        k, C = self.k, self.C
        for b in banks:
            k.mm(k.ps[b][:, 0:cols], C['zeros_bf'][:, 0:128], C['zeros_bf'][:, 0:cols], True, False,
                 ['const'], [('psb', b)])


def softmax_tiles(A, qT, qkey, m, tiles, obank, dbank, fox_bias=None):
    k, C = A.k, A.C
    pend = None
    n = len(tiles)

    def pv(p):
        t, pt, pkey, last = p
        nk, lo, hi = t['nk'], t['lo'], t['hi']
        k.mm(k.ps[obank][:, lo:hi], t['v'], pt[:nk, lo:hi], False, last, [t['vkey'], pkey], [('psb', obank)])
        if dbank is not None:
            k.mm(k.ps[dbank][:, lo:hi], C['ones_bf'][:nk, :], pt[:nk, lo:hi], False, last, ['const', pkey],
                 [('psb', dbank)])

    for ti, t in enumerate(tiles):
        bank, bkey = A.sc.next()
        nk, lo, hi = t['nk'], t['lo'], t['hi']
        adds = t.get('adds', [])
        k.mm(k.ps[bank][:nk, lo:hi], t['kT'], qT[:, m * 512 + lo:m * 512 + hi], True, len(adds) == 0,
             [t['kkey'], qkey], [bkey])
        for ai, (l_, r_, rows, clo, chi, keys) in enumerate(adds):
            k.mm(k.ps[bank][:rows, clo:chi], l_, r_, False, ai == len(adds) - 1, keys, [bkey])
        if pend is not None:
            pv(pend)
        pt, pkey = A.pt.next()
        if fox_bias is not None:
            for c in range(lo // 128, hi // 128):
                fbias, fkey = fox_bias(t['j'], m * 4 + c)
                k.act(pt[:nk, c * 128:(c + 1) * 128], k.ps[bank][:nk, c * 128:(c + 1) * 128], AF.Exp, [bkey, fkey],
                      [pkey], bias=fbias, scale=SCALE)
        else:
            k.act(pt[:nk, lo:hi], k.ps[bank][:nk, lo:hi], AF.Exp, [bkey], [pkey], scale=SCALE)
        pend = (t, pt, pkey, ti == n - 1)
    pv(pend)


def silu_parts(A, zt, zkey, m):
    k = A.k
    e1, e1k = A.ft.next()
    k.act(e1[:, :], zt[:, m * 512:(m + 1) * 512], AF.Exp, [zkey], [e1k], scale=-1.0)
    k.act(e1[:, :], e1[:, :], AF.Ln, [e1k], [e1k], bias=1.0, scale=1.0)
    return e1, e1k


def epilogue_single(A, zt, zkey, m, obank, dbank, gst, gkey, den_bias):
    k, C = A.k, A.C
    l1, l1k = silu_parts(A, zt, zkey, m)
    if dbank is not None:
        l2, l2k = A.ft.next()
        k.act(l2[:, :], k.ps[dbank][:, :], AF.Ln, [('psb', dbank), 'const', 'sinkcol'], [l2k], bias=den_bias, scale=1.0)
        k.tt('pool', l1[:, :], l1[:, :], l2[:, :], ALU.add, [l1k, l2k], [l1k])
    k.act(l1[:, :], l1[:, :], AF.Exp, [l1k], [l1k], scale=-1.0)
    t1, t1k = A.ft.next()
    k.tt('dve', t1[:, :], k.ps[obank][:, :], l1[:, :], ALU.mult, [('psb', obank), l1k], [t1k])
    k.tt('dve', gst[:, m * 512:(m + 1) * 512], t1[:, :], zt[:, m * 512:(m + 1) * 512], ALU.mult, [t1k, zkey],
         [(gkey, m)])


def load_head(k, bufs, keyname, idx, srcs):
    for t, d in srcs:
        k.dma('sp', t, d, [], [(keyname, idx)])


def phase_A_swa(k, C, D):
    nc = k.nc
    with ExitStack() as ph:
        A = ACtx(k, C, ph)
        qt = [sbt(nc, ph, "qt%d" % i, [128, SEQ], BF16) for i in range(2)]
        kt = [sbt(nc, ph, "kt%d" % i, [128, SEQ], BF16) for i in range(2)]
        vt = [sbt(nc, ph, "vt%d" % i, [128, NTB, 128], BF16) for i in range(2)]
        zt = [sbt(nc, ph, "zt%d" % i, [128, SEQ], F32) for i in range(2)]
        bt = [sbt(nc, ph, "bt%d" % i, [128, 2, NBT], BF16) for i in range(2)]
        sk = sbt(nc, ph, "sk", [128, NH], F32)
        k.tt('dve', sk[:], C['sinkb'][:], C['rb31'][:], ALU.subtract, ['const'], ['sinkcol'])
        k.act(sk[:], sk[:], AF.Exp, ['sinkcol'], ['sinkcol'])
        def loads(h):
            g = h // 4
            hb = h % 2
            gb = g % 2
            if h % 4 == 0:
                k.dma('sp', kt[gb][:], D['FMB'][16 + g], [], [('kt', gb)])
                k.dma('sp', vt[gb][:], D['VS'][g], [], [('vt', gb)])
            k.dma('sp', qt[hb][:], D['FMB'][h], [], [('qt', hb)])
            k.dma('sp', zt[hb][:], D['ZT'][h], [], [('zt', hb)])
            k.dma('sp', bt[hb][:], D['BTH'][h], [], [('bt', hb)])

        loads(0)
        for h in range(NH):
            g = h // 4
            hb = h % 2
            gb = g % 2
            if h + 1 < NH:
                loads(h + 1)
            gst, gkey = A.gst.next()
            ident = C['ident_bf']
            for m in range(4):
                (ob, db), _ = A.ob.next()
                A.zero_init([ob, db])
                tiles = []
                for j in range(max(0, 4 * m - 1), 4 * m + 4):
                    r = j - 4 * m
                    adds = []
                    if r == -1:
                        lo, hi = 0, 128
                        for hl in range(2):
                            adds.append((ident[:], bt[hb][:, hl, 256:384], 128, 0, 128, ['const', ('bt', hb)]))
                    else:
                        lo, hi = 128 * r, min(128 * (r + 2), 512)
                        for hl in range(2):
                            adds.append((ident[:], bt[hb][:, hl, 0:128], 128, lo, lo + 128, ['const', ('bt', hb)]))
                        if r < 3:
                            for hl in range(2):
                                adds.append((ident[:], bt[hb][:, hl, 256:384], 128, lo + 128, lo + 256,
                                             ['const', ('bt', hb)]))
                    tiles.append(dict(kT=kt[gb][:, j * 128:(j + 1) * 128], kkey=('kt', gb), nk=128, lo=lo, hi=hi,
                                      adds=adds, v=vt[gb][:, j, :], vkey=('vt', gb), j=j))
                softmax_tiles(A, qt[hb], ('qt', hb), m, tiles, ob, db)
                epilogue_single(A, zt[hb], ('zt', hb), m, ob, db, gst, gkey, sk[:, h:h + 1])
            k.dma('sp', D['G'][h], gst[:], [(gkey, m) for m in range(4)], [])
        k.S.flush()


def phase_A_fox(k, C, D):
    nc = k.nc
    with ExitStack() as ph:
        A = ACtx(k, C, ph)
        qt = [sbt(nc, ph, "qt%d" % i, [128, SEQ], BF16) for i in range(2)]
        kt = [sbt(nc, ph, "kt%d" % i, [128, SEQ], BF16) for i in range(2)]
        vt = [sbt(nc, ph, "vt%d" % i, [128, NTB, 128], BF16) for i in range(2)]
        zt = [sbt(nc, ph, "zt%d" % i, [128, SEQ], F32) for i in range(2)]
        fl = sbt(nc, ph, "fl", [128, NTB, NH], F32)
        pfx = sbt(nc, ph, "pfx", [128, NTB, NH], F32)
        cs = sbt(nc, ph, "cs", [128, NTB, NH], F32)
        csm = sbt(nc, ph, "csm", [128, NTB, NH], F32)
        bm = [sbt(nc, ph, "bm%d" % i, [128, NTB, NTB], F32) for i in range(2)]
        k.dma('sp', fl[:], D['FL'], [], ['fl'])
        fl2 = fl[:].rearrange("p a b -> p (a b)")
        k.tt('dve', fl2, fl2, C['fbb'][:], ALU.add, ['fl', 'const'], ['fl'])
        k.act(fl2, fl2, AF.Exp, ['fl'], ['fl'], scale=-1.0)
        k.act(fl2, fl2, AF.Ln, ['fl'], ['fl'], bias=1.0, scale=1.0)
        k.memset('dve', pfx[:, 0, :], 0.0, [('pfx', 0)])
        for tb in range(1, NTB):
            k.tt('dve', pfx[:, tb, :], pfx[:, tb - 1, :], fl[:, tb - 1, :], ALU.add, [('pfx', tb - 1), 'fl'],
                 [('pfx', tb)])
        pk = [('pfx', tb) for tb in range(NTB)]
        k.mm(k.ps[7][:, 0:256], C['tri_f'][:], fl2, True, False, ['const', 'fl'], [('psb', 7)])
        k.mm(k.ps[7][:, 0:256], C['ones_f'][:], pfx[:].rearrange("p a b -> p (a b)"), False, True, ['const'] + pk,
             [('psb', 7)])
        k.cp('dve', cs[:].rearrange("p a b -> p (a b)"), k.ps[7][:, 0:256], [('psb', 7)], ['cs'])
        k.mm(k.ps[7][:, 256:512], C['sel64_f'][:], cs[:].rearrange("p a b -> p (a b)"), True, True, ['const', 'cs'],
             [('psb', 7)])
        k.cp('dve', csm[:].rearrange("p a b -> p (a b)"), k.ps[7][:, 256:512], [('psb', 7)], ['csm'])
        def loads(h):
            hb = h % 2
            k.dma('sp', kt[hb][:], D['FMB'][16 + h], [], [('kt', hb)])
            k.dma('sp', vt[hb][:], D['VS'][h], [], [('vt', hb)])
            k.dma('sp', qt[hb][:], D['FMB'][h], [], [('qt', hb)])
            k.dma('sp', zt[hb][:], D['ZT'][h], [], [('zt', hb)])

        loads(0)
        for h in range(DBG_HEADS):
            hb = h % 2
            if h + 1 < NH:
                loads(h + 1)
            for i in range(NTB):
                k.ts('dve', bm[hb][:, i, :], cs[:, :, h], csm[:, i, h:h + 1], ALU.subtract, ['cs', 'csm'],
                     [('bm', hb)])
            gst, gkey = A.gst.next()
            fb = (lambda hb_: (lambda j, i: (bm[hb_][:, i, j:j + 1], ('bm', hb_))))(hb)
            for m in range(4):
                (ob, db), _ = A.ob.next()
                A.zero_init([ob, db])
                tiles = []
                for j in range(0, 4 * m + 4):
                    r = j - 4 * m
                    adds = []
                    lo, hi = (0, 512) if r < 0 else (128 * r, 512)
                    if r >= 0:
                        adds.append((C['ident_bf'][:], C['mcausal'][:], 128, lo, lo + 128, ['const']))
                    tiles.append(dict(kT=kt[hb][:, j * 128:(j + 1) * 128], kkey=('kt', hb), nk=128, lo=lo, hi=hi,
                                      adds=adds, v=vt[hb][:, j, :], vkey=('vt', hb), j=j))
                softmax_tiles(A, qt[hb], ('qt', hb), m, tiles, ob, db, fox_bias=fb)
                epilogue_single(A, zt[hb], ('zt', hb), m, ob, db, gst, gkey, C['tinycol'][:])
            k.dma('sp', D['G'][h], gst[:], [(gkey, m) for m in range(4)], [])
            if h % 4 == 3 and h + 1 < NH:
                k.S.flush()
        k.S.flush()


def phase_A_sb(k, C, D):
    nc = k.nc
    with ExitStack() as ph:
        A = ACtx(k, C, ph, n_pt=3)
        qt = [sbt(nc, ph, "qt%d" % i, [128, SEQ], BF16) for i in range(2)]
        kt = [sbt(nc, ph, "kt%d" % i, [128, SEQ], BF16) for i in range(2)]
        vt = [sbt(nc, ph, "vt%d" % i, [128, NTB, 128], BF16) for i in range(2)]
        zt = [sbt(nc, ph, "zt%d" % i, [128, SEQ], F32) for i in range(2)]
        spr = Ring('spt', [sbt(nc, ph, "spt%d" % i, [128, 512], F32) for i in range(3)])
        e1r = Ring('e1t', [sbt(nc, ph, "e1t%d" % i, [128, 512], F32) for i in range(3)])
        spacc = [sbt(nc, ph, "spacc%d" % i, [128, 512], F32) for i in range(2)]
        zring = Ring('psb', [0, 1])
        aring = Ring('psb', [2, 7])
        obanks = [3, 5]
        cnt = 0
        def loads(h):
            hb = h % 2
            k.dma('sp', kt[hb][:], D['FMB'][16 + h], [], [('kt', hb)])
            k.dma('sp', vt[hb][:], D['VS'][h], [], [('vt', hb)])
            k.dma('sp', qt[hb][:], D['FMB'][h], [], [('qt', hb)])
            k.dma('sp', zt[hb][:], D['ZT'][h], [], [('zt', hb)])

        loads(0)
        for h in range(NH):
            hb = h % 2
            if h + 1 < NH:
                loads(h + 1)
            gst, gkey = A.gst.next()
            for m in range(4):
                ob = obanks[cnt % 2]
                sa = spacc[cnt % 2]
                sakey = ('spacc', cnt % 2)
                cnt += 1
                A.zero_init([ob])
                k.memset('pool', sa[:, :], 0.0, [sakey])
                js = list(range(4 * m + 3, -1, -1))
                n = len(js)
                st = [None] * n

                def stage1(i):
                    j = js[i]
                    r = j - 4 * m
                    lo = 0 if r < 0 else 128 * r
                    zb_, zkey_ = zring.next()
                    zb = zb_
                    zkey = ('psb', zb)
                    k.mm(k.ps[zb][:, lo:512], kt[hb][:, j * 128:(j + 1) * 128], qt[hb][:, m * 512 + lo:(m + 1) * 512],
                         True, True, [('kt', hb), ('qt', hb)], [zkey])
                    sp, spk = spr.next()
                    k.act(sp[:, lo:512], k.ps[zb][:, lo:512], AF.Exp, [zkey], [spk], scale=SCALE)
                    k.act(sp[:, lo:512], sp[:, lo:512], AF.Ln, [spk], [spk], bias=1.0, scale=1.0)
                    if r >= 0:
                        k.tt('pool', sp[:, lo:lo + 128], sp[:, lo:lo + 128], C['mstrict_f'][:], ALU.mult,
                             [spk, 'const'], [spk])
                    st[i] = dict(j=j, r=r, lo=lo, zb=zb, zkey=zkey, sp=sp, spk=spk)

                def stage2(i):
                    s = st[i]
                    lo, sp, spk = s['lo'], s['sp'], s['spk']
                    ab_, _ = aring.next()
                    ab = ab_
                    akey = ('psb', ab)
                    first = (i == 0)
                    diag = s['r'] >= 0
                    k.mm(k.ps[ab][:, lo:512], C['ustrict_f'][:], sp[:, lo:512], True, first and not diag,
                         ['const', spk], [akey])
                    if not first:
                        k.mm(k.ps[ab][:, lo:512], C['ones_f'][:], sa[:, lo:512], False, not diag, ['const', sakey],
                             [akey])
                    if diag:
                        k.mm(k.ps[ab][:, lo:lo + 128], C['ident_bf'][:], C['mpos_bf'][:], False, True, ['const'],
                             [akey])
                    if i < n - 1:
                        k.tt('pool', sa[:, lo:512], sa[:, lo:512], sp[:, lo:512], ALU.add, [sakey, spk], [sakey])
                    e1, e1k = e1r.next()
                    k.stt(e1[:, lo:512], k.ps[s['zb']][:, lo:512], SCALE, sp[:, lo:512], ALU.mult, ALU.subtract,
                          [s['zkey'], spk], [e1k])
                    k.tt('dve', e1[:, lo:512], e1[:, lo:512], k.ps[ab][:, lo:512], ALU.subtract, [e1k, akey], [e1k])
                    pt, pkey = A.pt.next()
                    k.act(pt[:, lo:512], e1[:, lo:512], AF.Exp, [e1k], [pkey])
                    s['pt'] = pt
                    s['pkey'] = pkey

                def stage3(i):
                    s = st[i]
                    lo = s['lo']
                    k.mm(k.ps[ob][:, lo:512], vt[hb][:, s['j'], :], s['pt'][:, lo:512], False, i == n - 1,
                         [('vt', hb), s['pkey']], [('psb', ob)])

                for step in range(n + 2):
                    if step < n:
                        stage1(step)
                    if 0 <= step - 1 < n:
                        stage2(step - 1)
                    if 0 <= step - 2 < n:
                        stage3(step - 2)
                epilogue_single(A, zt[hb], ('zt', hb), m, ob, None, gst, gkey, None)
            k.dma('sp', D['G'][h], gst[:], [(gkey, m) for m in range(4)], [])
            if h % 4 == 3 and h + 1 < NH:
                k.S.flush()
        k.S.flush()


NSB_LAYOUT = [('shiftm', 512), ('blksel', 2048), ('interA', 64)]


def nsa_inputs(nc, dt_in):
    return dict(
        w1k=dt_in("w1k", [128, 32, 128]).ap(), w1v=dt_in("w1v", [128, 32, 128]).ap(),
        w2k=dt_in("w2k", [128, 128]).ap(), w2v=dt_in("w2v", [128, 128]).ap(),
        pek=dt_in("pek", [128, 32]).ap(), pev=dt_in("pev", [128, 32]).ap(),
        nsb=dt_in("nsb", [128, sum(n for _, n in NSB_LAYOUT)]).ap(),
        mulc=dt_in("mulc", [128, 512]).ap(), addc=dt_in("addc", [128, 512]).ap(),
    )


def nsa_host_inputs(inputs):
    f32 = np.float32
    m = {}
    m['w1k'] = np.ascontiguousarray(np.transpose(np.asarray(inputs['cmp_w1_k'], f32), (1, 0, 2)))
    m['w1v'] = np.ascontiguousarray(np.transpose(np.asarray(inputs['cmp_w1_v'], f32), (1, 0, 2)))
    m['w2k'] = np.ascontiguousarray(inputs['cmp_w2_k'], f32)
    m['w2v'] = np.ascontiguousarray(inputs['cmp_w2_v'], f32)
    m['pek'] = np.ascontiguousarray(np.asarray(inputs['cmp_pe_k'], f32).T)
    m['pev'] = np.ascontiguousarray(np.asarray(inputs['cmp_pe_v'], f32).T)
    shiftm = np.zeros((128, 512), f32)
    for mm_ in range(4):
        for kk in range(40):
            n = 32 * mm_ - 9 + kk
            if 0 <= n < 127:
                shiftm[kk, mm_ * 128 + n] = 1.0
    blksel = np.zeros((128, 2048), f32)
    for j in range(16):
        for s_ in range(128):
            blksel[2 * j + (1 if s_ >= 64 else 0), j * 128 + s_] = 1.0
    n_ = np.arange(127)
    jj = np.arange(32)
    inter = np.clip(np.minimum(n_[:, None] * 16 + 32, jj[None, :] * 64 + 64)
                    - np.maximum(n_[:, None] * 16, jj[None, :] * 64), 0, None) / 32.0
    interA = np.zeros((128, 64), f32)
    interA[:127, :32] = inter
    interA[:127, 32] = 1.0
    m['nsb'] = np.ascontiguousarray(np.concatenate([shiftm, blksel, interA], axis=1))
    t = np.arange(SEQ)
    allowed = (jj[None, :] * 64 <= t[:, None])
    cur = t // 64
    forced = (jj[None, :] == 0) | (jj[None, :] == cur[:, None]) | (jj[None, :] == cur[:, None] - 1)
    mulc = (allowed & ~forced).astype(f32)
    addc = np.where(forced & allowed, 1e4, np.where(allowed, 0.0, -1e30)).astype(f32)
    m['mulc'] = np.ascontiguousarray(mulc.reshape(NTB, 128, 32).transpose(1, 0, 2).reshape(128, 512))
    m['addc'] = np.ascontiguousarray(addc.reshape(NTB, 128, 32).transpose(1, 0, 2).reshape(128, 512))
    return m


def phase_A_nsa(k, C, D, NS):
    nc = k.nc
    FMB, VS, ZT, GL, BTH, G = D['FMB'], D['VS'], D['ZT'], D['GL'], D['BTH'], D['G']
    with ExitStack() as ph:
        A = ACtx(k, C, ph)
        w1 = [sbt(nc, ph, "w1_%d" % i, [128, 32, 128], BF16) for i in range(2)]
        w2 = [sbt(nc, ph, "w2_%d" % i, [128, 128], BF16) for i in range(2)]
        pe = [sbt(nc, ph, "pe_%d" % i, [128, 32], BF16) for i in range(2)]
        bcol = sbt(nc, ph, "bcol", [128, 4], F32)
        nsb = sbt(nc, ph, "nsb", [128, sum(n for _, n in NSB_LAYOUT)], BF16)
        mulc = sbt(nc, ph, "mulc", [128, 512], F32)
        addc = sbt(nc, ph, "addc", [128, 512], F32)
        shiftm = nsb[:, 0:512]
        blksel = nsb[:, 512:2560]
        interA = nsb[:, 2560:2624]
        glt = sbt(nc, ph, "glt", [128, SEQ], F32)
        ct = [sbt(nc, ph, "ct%d" % i, [128, SEQ], BF16) for i in range(2)]
        kst = sbt(nc, ph, "kst", [128, SEQ], BF16)
        kwt = sbt(nc, ph, "kwt", [128, SEQ], BF16)
        vst = sbt(nc, ph, "vst", [128, NTB, 128], BF16)
        vwt = sbt(nc, ph, "vwt", [128, NTB, 128], BF16)
        hid = [sbt(nc, ph, "hid%d" % i, [128, 128], BF16) for i in range(2)]
        kcmp = sbt(nc, ph, "kcmp", [128, 128], BF16)
        vcmp = sbt(nc, ph, "vcmp", [128, 128], BF16)
        qt = [sbt(nc, ph, "qt%d" % i, [128, SEQ], BF16) for i in range(2)]
        zt = [sbt(nc, ph, "zt%d" % i, [128, SEQ], F32) for i in range(2)]
        bt = [sbt(nc, ph, "bt%d" % i, [128, 2, NBT], BF16) for i in range(2)]
        gat = [sbt(nc, ph, "gat%d" % i, [128, 3, 512], F32) for i in range(2)]
        acc = [sbt(nc, ph, "acc%d" % i, [128, 512], F32) for i in range(2)]
        impa = sbt(nc, ph, "impa", [128, NTB, 32], F32)
        rec = sbt(nc, ph, "rec", [128, 4], F32)
        top8 = sbt(nc, ph, "top8", [128, 8], F32)
        nsl = sbt(nc, ph, "nsl", [128, NTB, 32], F32)
        nselT = sbt(nc, ph, "nselT", [32, SEQ], BF16)
        ident = C['ident_bf']

        k.dma('pool', w1[0][:], NS['w1k'], [], ['w1'])
        k.dma('pool', w1[1][:], NS['w1v'], [], ['w1'])
        k.dma('pool', w2[0][:], NS['w2k'], [], ['w1'])
        k.dma('pool', w2[1][:], NS['w2v'], [], ['w1'])
        k.dma('pool', pe[0][:], NS['pek'], [], ['w1'])
        k.dma('pool', pe[1][:], NS['pev'], [], ['w1'])
        k.dma('pool', nsb[:], NS['nsb'], [], ['nsb'])
        k.dma('sp', mulc[:], NS['mulc'], [], ['mulc'])
        k.dma('sp', addc[:], NS['addc'], [], ['mulc'])
        for i in range(2):
            for l in range(32):
                k.mm(k.ps[7][:, i:i + 1], w1[i][:, l, :], pe[i][:, l:l + 1], l == 0, l == 31, ['w1'], [('psb', 7)])
        k.cp('dve', bcol[:, 0:1], k.ps[7][:, 0:1], [('psb', 7)], ['bcol'])
        k.cp('dve', bcol[:, 2:3], k.ps[7][:, 1:2], [('psb', 7)], ['bcol'])
        k.ts('dve', bcol[:, 1:2], k.ps[7][:, 0:1], -1.0, ALU.mult, [('psb', 7), 'bcol'], ['bcol'])
        k.ts('dve', bcol[:, 3:4], k.ps[7][:, 1:2], -1.0, ALU.mult, [('psb', 7), 'bcol'], ['bcol'])
        k.dma('sp', glt[:48, :], GL[0:48], [], ['glt'])
        k.act(glt[:48, :], glt[:48, :], AF.Exp, ['glt'], ['glt'], scale=-1.0)
        k.act(glt[:48, :], glt[:48, :], AF.Ln, ['glt'], ['glt'], bias=1.0, scale=1.0)
        k.act(glt[:48, :], glt[:48, :], AF.Exp, ['glt'], ['glt'], scale=-1.0)
        k.dma('sp', GL[0:48], glt[:48, :], ['glt'], [])
        k.S.flush()

        def cmp_tile(hb, m):
            nk = min(127, 32 * m + 31)
            adds = []
            for hl in range(2):
                adds.append((shiftm[:40, m * 128:m * 128 + nk], bt[hb][:40, hl, 384:896], nk, 0, 512,
                             ['nsb', ('bt', hb)]))
            return dict(kT=kcmp[:, :nk], kkey='kcmp', nk=nk, lo=0, hi=512, adds=adds, v=vcmp[:nk, :], vkey='vcmp',
                        j=0)

        for g in range(NKV):
            k.dma('sp', ct[0][:], FMB[16 + g], [], [('ct', 0)])
            k.dma('sp', ct[1][:], FMB[20 + g], [], [('ct', 1)])
            k.dma('sp', kst[:], FMB[24 + g], [], ['kst'])
            k.dma('sp', kwt[:], FMB[28 + g], [], ['kwt'])
            k.dma('sp', vst[:], VS[g], [], ['vst'])
            k.dma('sp', vwt[:], VS[4 + g], [], ['vwt'])
            for i in range(2):
                for l in range(32):
                    k.mm(k.ps[7][:, 0:127], w1[i][:, l, :], ct[i][:, l:l + 16 * 126 + 1:16], l == 0, l == 31,
                         ['w1', ('ct', i)], [('psb', 7)])
                f1, f1k = A.ft.next()
                k.act(f1[:, 0:127], k.ps[7][:, 0:127], AF.Exp, [('psb', 7), 'bcol'], [f1k],
                      bias=bcol[:, 2 * i + 1:2 * i + 2], scale=-1.0)
                k.act(f1[:, 0:127], f1[:, 0:127], AF.Ln, [f1k], [f1k], bias=1.0, scale=1.0)
                k.act(f1[:, 0:127], f1[:, 0:127], AF.Exp, [f1k], [f1k], scale=-1.0)
                k.stt(hid[i][:, 0:127], k.ps[7][:, 0:127], bcol[:, 2 * i:2 * i + 1], f1[:, 0:127], ALU.add, ALU.mult,
                      [('psb', 7), 'bcol', f1k], [('hid', i)])
                if i == 0:
                    k.mm(k.ps[7][:, 128:255], w2[0][:], hid[0][:, 0:127], True, True, ['w1', ('hid', 0)],
                         [('psb', 7)])
                    k.cp('dve', kcmp[:, 0:127], k.ps[7][:, 128:255], [('psb', 7)], ['kcmp'])
                else:
                    k.mm(k.ps[7][:127, 256:384], hid[1][:, 0:127], w2[1][:], True, True, ['w1', ('hid', 1)],
                         [('psb', 7)])
                    k.cp('dve', vcmp[:127, :], k.ps[7][:127, 256:384], [('psb', 7)], ['vcmp'])
            for hi_ in range(4):
                h = 4 * g + hi_
                hb = h % 2
                k.dma('sp', qt[hb][:], FMB[h], [], [('qt', hb)])
                k.dma('sp', bt[hb][:], BTH[h], [], [('bt', hb)])
                for m in range(4):
                    t = cmp_tile(hb, m)
                    nk = t['nk']
                    bank, bkey = A.sc.next()
                    k.mm(k.ps[bank][:nk, 0:512], t['kT'], qt[hb][:, m * 512:(m + 1) * 512], True, False,
                         ['kcmp', ('qt', hb)], [bkey])
                    for ai, (l_, r_, rows, clo, chi, keys) in enumerate(t['adds']):
                        k.mm(k.ps[bank][:rows, clo:chi], l_, r_, False, ai == 1, keys, [bkey])
                    pt, pkey = A.pt.next()
                    k.act(pt[:nk, 0:512], k.ps[bank][:nk, 0:512], AF.Exp, [bkey], [pkey], scale=SCALE)
                    for c in range(4):
                        k.mm(k.ps[7][:, c * 33:c * 33 + 33], pt[:nk, c * 128:(c + 1) * 128], interA[:nk, 0:33], True,
                             True, [pkey, 'nsb'], [('psb', 7)])
                    k.ts('dve', rec[:, 0:4], k.ps[7][:, 32:32 + 33 * 3 + 1:33], 1e-30, ALU.max, [('psb', 7)], ['rec'])
                    k.S.op('dve', lambda e: e.reciprocal(out=rec[:, 0:4], in_=rec[:, 0:4]), ['rec'], ['rec'])
                    for c in range(4):
                        tb = 4 * m + c
                        if hi_ == 0:
                            k.ts('dve', impa[:, tb, :], k.ps[7][:, c * 33:c * 33 + 32], rec[:, c:c + 1], ALU.mult,
                                 [('psb', 7), 'rec'], [('impa', tb)])
                        else:
                            k.stt(impa[:, tb, :], k.ps[7][:, c * 33:c * 33 + 32], rec[:, c:c + 1], impa[:, tb, :],
                                  ALU.mult, ALU.add, [('psb', 7), 'rec', ('impa', tb)], [('impa', tb)])
            ik = [('impa', tb) for tb in range(NTB)]
            ia = impa[:].rearrange("p a b -> p (a b)")
            k.tt('dve', ia, ia, mulc[:], ALU.mult, ik + ['mulc'], ik)
            k.tt('dve', ia, ia, addc[:], ALU.add, ik + ['mulc'], ik)
            for tb in range(NTB):
                k.S.op('dve', (lambda o, i_: (lambda e: e.max(out=o, in_=i_)))(top8[:, :], impa[:, tb, :]),
                       [('impa', tb)], ['top8'])
                k.ts('dve', top8[:, 7:8], top8[:, 7:8], -5e29, ALU.max, ['top8'], ['top8'])
                k.ts('dve', nsl[:, tb, :], impa[:, tb, :], top8[:, 7:8], ALU.is_ge, [('impa', tb), 'top8'],
                     [('nsl', tb)])
                k.ts('dve', nsl[:, tb, :], nsl[:, tb, :], -1.0, ALU.add, [('nsl', tb)], [('nsl', tb)], s2=-NEG,
                     op1=ALU.mult)
            for q4 in range(4):
                for u in range(4):
                    tb = 4 * q4 + u
                    k.tr(k.ps[7][:32, u * 128:(u + 1) * 128], nsl[:, tb, :], C['ident_f'][:], [('nsl', tb), 'const'],
                         [('psb', 7)])
                k.cp('dve', nselT[:, q4 * 512:(q4 + 1) * 512], k.ps[7][:32, :], [('psb', 7)], ['nselT'])
            k.S.flush()
            cnt = 0
            for hi_ in range(4):
                h = 4 * g + hi_
                hb = h % 2
                k.dma('sp', qt[hb][:], FMB[h], [], [('qt', hb)])
                k.dma('sp', bt[hb][:], BTH[h], [], [('bt', hb)])
                k.dma('sp', zt[hb][:], ZT[h], [], [('zt', hb)])
                gst, gkey = A.gst.next()
                for m in range(4):
                    gb = cnt % 2
                    cnt += 1
                    k.dma('sp', gat[gb][:], GL[3 * h:3 * h + 3, m * 512:(m + 1) * 512].partition_broadcast(128), [],
                          [('gat', gb)])
                    ac = acc[gb]
                    ackey = ('acc', gb)
                    for br in range(3):
                        (ob, db), _ = A.ob.next()
                        A.zero_init([ob, db])
                        if br == 0:
                            tiles = [cmp_tile(hb, m)]
                        elif br == 1:
                            tiles = []
                            for j in range(0, 4 * m + 4):
                                r = j - 4 * m
                                lo = 0 if r < 0 else 128 * r
                                adds = [(blksel[:32, j * 128:(j + 1) * 128], nselT[:32, m * 512 + lo:(m + 1) * 512],
                                         128, lo, 512, ['nsb', 'nselT'])]
                                if r >= 0:
                                    for hl in range(2):
                                        adds.append((ident[:], bt[hb][:, hl, 0:128], 128, lo, lo + 128,
                                                     ['const', ('bt', hb)]))
                                if -1 <= r < 3:
                                    for hl in range(2):
                                        adds.append((ident[:], bt[hb][:, hl, 128:256], 128, 128 * (r + 1),
                                                     128 * (r + 2), ['const', ('bt', hb)]))
                                tiles.append(dict(kT=kst[:, j * 128:(j + 1) * 128], kkey='kst', nk=128, lo=lo, hi=512,
                                                  adds=adds, v=vst[:, j, :], vkey='vst', j=j))
                        else:
                            tiles = []
                            for j in range(max(0, 4 * m - 4), 4 * m + 4):
                                r = j - 4 * m
                                adds = []
                                if r >= 0:
                                    lo, hi = 128 * r, 512
                                    for hl in range(2):
                                        adds.append((ident[:], bt[hb][:, hl, 0:128], 128, lo, lo + 128,
                                                     ['const', ('bt', hb)]))
                                else:
                                    lo, hi = 0, 128 * (5 + r)
                                    adds.append((ident[:], C['md4'][:], 128, 128 * (4 + r), 128 * (5 + r), ['const']))
                                if -1 <= r < 3:
                                    for hl in range(2):
                                        adds.append((ident[:], bt[hb][:, hl, 128:256], 128, 128 * (r + 1),
                                                     128 * (r + 2), ['const', ('bt', hb)]))
                                tiles.append(dict(kT=kwt[:, j * 128:(j + 1) * 128], kkey='kwt', nk=128, lo=lo, hi=hi,
                                                  adds=adds, v=vwt[:, j, :], vkey='vwt', j=j))
                        softmax_tiles(A, qt[hb], ('qt', hb), m, tiles, ob, db)
                        l2, l2k = A.ft.next()
                        k.act(l2[:, :], k.ps[db][:, :], AF.Ln, [('psb', db), 'const'], [l2k], bias=C['tinycol'][:],
                              scale=1.0)
                        k.act(l2[:, :], l2[:, :], AF.Exp, [l2k], [l2k], scale=-1.0)
                        k.tt('pool', l2[:, :], l2[:, :], gat[gb][:, br, :], ALU.mult, [l2k, ('gat', gb)], [l2k])
                        if br == 0:
                            k.tt('dve', ac[:, :], k.ps[ob][:, :], l2[:, :], ALU.mult, [('psb', ob), l2k], [ackey])
                        else:
                            k.tt('dve', l2[:, :], k.ps[ob][:, :], l2[:, :], ALU.mult, [('psb', ob), l2k], [l2k])
                            k.tt('pool', ac[:, :], ac[:, :], l2[:, :], ALU.add, [ackey, l2k], [ackey])
                    l1, l1k = silu_parts(A, zt[hb], ('zt', hb), m)
                    k.act(l1[:, :], l1[:, :], AF.Exp, [l1k], [l1k], scale=-1.0)
                    k.tt('pool', l1[:, :], l1[:, :], zt[hb][:, m * 512:(m + 1) * 512], ALU.mult, [l1k, ('zt', hb)],
                         [l1k])
                    k.tt('dve', gst[:, m * 512:(m + 1) * 512], ac[:, :], l1[:, :], ALU.mult, [ackey, l1k],
                         [(gkey, m)])
                k.dma('sp', G[h], gst[:], [(gkey, m) for m in range(4)], [])
                k.S.flush()


CF_NAMES = ['ident_f', 'ones_f', 'tri_f', 'sel64_f', 'ustrict_f', 'mstrict_f']
CB_LAYOUT = [('ident_bf', 128), ('ones_bf', 128), ('zeros_bf', 512), ('mcausal', 128), ('mpos_bf', 128), ('md4', 128)]
CS_LAYOUT = [('epscol', 1), ('tinycol', 1), ('rb31', 16), ('sinkb', 16), ('fbb', 256)]


def rel_bucket_np(dist):
    dist = np.maximum(dist, 0)
    ratio = np.maximum(dist, 16).astype(np.float32) / np.float32(16)
    large = 16 + (np.log(ratio) / np.float32(np.log(8.0)) * np.float32(16)).astype(np.int32)
    return np.where(dist < 16, dist, np.minimum(large, 31)).astype(np.int64)


def host_consts():
    p = np.arange(128)
    I = np.eye(128, dtype=np.float32)
    cf = {
        'ident_f': I,
        'ones_f': np.ones((128, 128), np.float32),
        'tri_f': (p[:, None] <= p[None, :]).astype(np.float32),
        'sel64_f': np.zeros((128, 128), np.float32),
        'ustrict_f': (p[:, None] > p[None, :]).astype(np.float32),
        'mstrict_f': (p[:, None] < p[None, :]).astype(np.float32),
    }
    cf['sel64_f'][64, :] = 1.0
    cb = {
        'ident_bf': I,
        'ones_bf': np.ones((128, 128), np.float32),
        'zeros_bf': np.zeros((128, 512), np.float32),
        'mcausal': np.where(p[:, None] > p[None, :], NEG, 0.0).astype(np.float32),
        'mpos_bf': np.where(p[:, None] >= p[None, :], -NEG, 0.0).astype(np.float32),
        'md4': np.where(p[None, :] < p[:, None], 0.0, NEG).astype(np.float32),
    }
    cfa = np.concatenate([cf[n] for n in CF_NAMES], axis=1)
    cba = np.concatenate([cb[n] for n, _ in CB_LAYOUT], axis=1)
    return np.ascontiguousarray(cfa), np.ascontiguousarray(cba)


def bias_gather_indices():
    p = np.arange(128)[:, None]
    f = np.arange(128)[None, :]
    idx = np.zeros((128, NBT), np.int64)
    msk = np.zeros((128, NBT), np.float32)
    d0 = f - p
    idx[:, 0:128] = rel_bucket_np(d0)
    msk[:, 0:128] = np.where(d0 >= 0, 0.0, NEG)
    d1 = 128 + f - p
    idx[:, 128:256] = rel_bucket_np(d1)
    idx[:, 256:384] = rel_bucket_np(d1)
    msk[:, 256:384] = np.where(d1 < 128, 0.0, NEG)
    kk = np.arange(128)[:, None]
    ff = np.arange(512)[None, :]
    dc = ff - 16 * kk + 113
    idx[:, 384:896] = rel_bucket_np(dc)
    msk[:, 384:896] = np.where(dc >= 0, 0.0, NEG)
    msk[40:, 384:896] = 0.0
    return idx, msk


def phase_bias(k, C, btg, bmask, BTH):
    nc = k.nc
    with ExitStack() as ph:
        bm = sbt(nc, ph, "bmk", [128, NBT], F32)
        bg = [sbt(nc, ph, "bg%d" % i, [128, NBT], F32) for i in range(2)]
        bo = [sbt(nc, ph, "bo%d" % i, [128, 2, NBT], BF16) for i in range(2)]
        k.dma('sp', bm[:], bmask, [], ['bmk'])
        for h in range(NH):
            b = h % 2
            k.dma('sp', bg[b][:], btg[h], [], [('bg', b)])
            k.ts('dve', bg[b][:], bg[b][:], C['rb31'][:, h:h + 1], ALU.subtract, [('bg', b), 'const'], [('bg', b)],
                 s2=1.0 / SCALE, op1=ALU.mult)
            k.tt('dve', bg[b][:], bg[b][:], bm[:], ALU.add, [('bg', b), 'bmk'], [('bg', b)])
            k.cp('dve', bo[b][:, 0, :], bg[b][:], [('bg', b)], [('bo', b)])
            k.tt('dve', bo[b][:, 1, :], bg[b][:], bo[b][:, 0, :], ALU.subtract, [('bg', b), ('bo', b)], [('bo', b)])
            k.dma('sp', BTH[h], bo[b][:], [('bo', b)], [])
        k.S.flush()


LAYER_KIND = ['nsa', 'swa', 'sb', 'fox']
WIN_COLS = [7216, 5120, 8192, 8208]


def build_program(layers, dbg=False):
    nc = bass.Bass("TRN2", target_bir_lowering=False)
    dt_in = lambda name, shape: nc.dram_tensor(name, shape, F32, kind="ExternalInput")
    x_in = dt_in("x", [SEQ, DM]).ap()
    x_out = nc.dram_tensor("xo", [SEQ, DM], F32, kind="ExternalOutput").ap()
    cfa = dt_in("cfa", [128, 128 * len(CF_NAMES)]).ap()
    cba = dt_in("cba", [128, sum(n for _, n in CB_LAYOUT)]).ap()
    csa = dt_in("csa", [128, sum(n for _, n in CS_LAYOUT)]).ap()
    LW = {}
    for l in layers:
        LW[l] = dict(w_in=dt_in("w_in%d" % l, [DM, WIN_COLS[l]]).ap(), w_out=dt_in("w_out%d" % l, [DM, DM]).ap(),
                     ln_g=dt_in("ln_g%d" % l, [DM]).ap(), ln_b=dt_in("ln_b%d" % l, [DM]).ap())
    need_bias = any(l in (0, 1) for l in layers)
    if need_bias:
        btg = dt_in("btg", [NH, 128, NBT])
        bmask = dt_in("bmask", [128, NBT]).ap()
    NS = {}
    if 0 in layers:
        NS = nsa_inputs(nc, dt_in)
    sk = "ExternalOutput" if dbg else "Internal"
    D = dict(
        XR=nc.dram_tensor("XR", [SEQ, DM], F32, kind="Internal").ap(),
        FMB=nc.dram_tensor("FMB", [32, 128, SEQ], BF16, kind=sk),
        VS=nc.dram_tensor("VS", [16, 128, NTB, 128], BF16, kind=sk),
        ZT=nc.dram_tensor("ZT", [16, 128, SEQ], F32, kind=sk),
        GL=nc.dram_tensor("GL", [48, SEQ], F32, kind=sk),
        FL=nc.dram_tensor("FL", [128, NTB, 16], F32, kind=sk).ap(),
        G=nc.dram_tensor("G", [16, 128, SEQ], BF16, kind=sk),
        BTH=nc.dram_tensor("BTH", [16, 128, 2, NBT], BF16, kind=sk),
    )
    with ExitStack() as es:
        k = K(nc, es)
        cf_sb = sbt(nc, es, "cf_sb", [128, 128 * len(CF_NAMES)], F32)
        cb_sb = sbt(nc, es, "cb_sb", [128, sum(n for _, n in CB_LAYOUT)], BF16)
        cs_sb = sbt(nc, es, "cs_sb", [128, sum(n for _, n in CS_LAYOUT)], F32)
        xT = sbt(nc, es, "xT", [128, 16, SEQ], BF16)
        C = {'xT': xT}
        for i, n in enumerate(CF_NAMES):
            C[n] = cf_sb[:, i * 128:(i + 1) * 128]
        o = 0
        for n, w in CB_LAYOUT:
            C[n] = cb_sb[:, o:o + w]
            o += w
        o = 0
        for n, w in CS_LAYOUT:
            C[n] = cs_sb[:, o:o + w]
            o += w
        k.dma('sp', cf_sb[:], cfa, [], ['const'])
        k.dma('pool', cb_sb[:], cba, [], ['const'])
        k.dma('sp', cs_sb[:], csa, [], ['const'])
        k.S.flush()
        if need_bias:
            phase_bias(k, C, btg, bmask, D['BTH'])
        phase_x0(k, C, x_in, D['XR'])
        for li, l in enumerate(layers):
            W = LW[l]
            kind = LAYER_KIND[l]
            FMB, VS, ZT = D['FMB'], D['VS'], D['ZT']
            groups = []
            if kind == 'nsa':
                for i in range(4):
                    groups.append(('FMB', 512 * i, 512, [FMB[4 * i + j] for j in range(4)]))
                for i, base in enumerate([16, 20, 24]):
                    groups.append(('FMB', 2048 + 512 * i, 512, [FMB[base + j] for j in range(4)]))
                groups.append(('TMV', 3584, 512, [VS[j] for j in range(4)]))
                groups.append(('FMB', 4096, 512, [FMB[28 + j] for j in range(4)]))
                groups.append(('TMV', 4608, 512, [VS[4 + j] for j in range(4)]))
                groups.append(('FMF', 5120, 48, [D['GL'][0:48]]))
                for i in range(4):
                    groups.append(('FMF', 5168 + 512 * i, 512, [ZT[4 * i + j] for j in range(4)]))
            elif kind == 'swa':
                for i in range(4):
                    groups.append(('FMB', 512 * i, 512, [FMB[4 * i + j] for j in range(4)]))
                groups.append(('FMB', 2048, 512, [FMB[16 + j] for j in range(4)]))
                groups.append(('TMV', 2560, 512, [VS[j] for j in range(4)]))
                for i in range(4):
                    groups.append(('FMF', 3072 + 512 * i, 512, [ZT[4 * i + j] for j in range(4)]))
            else:
                for i in range(8):
                    groups.append(('FMB', 512 * i, 512, [FMB[4 * i + j] for j in range(4)]))
                for i in range(4):
                    groups.append(('TMV', 4096 + 512 * i, 512, [VS[4 * i + j] for j in range(4)]))
                zc = 6144
                if kind == 'fox':
                    groups.append(('TMF', 6144, 16, D['FL']))
                    zc = 6160
                for i in range(4):
                    groups.append(('FMF', zc + 512 * i, 512, [ZT[4 * i + j] for j in range(4)]))
            phase_P(k, C, W['w_in'], groups)
            if kind == 'nsa':
                phase_A_nsa(k, C, D, NS)
            elif kind == 'swa':
                phase_A_swa(k, C, D)
            elif kind == 'sb':
                phase_A_sb(k, C, D)
            else:
                phase_A_fox(k, C, D)
            last = (li == len(layers) - 1)
            phase_B(k, C, W['w_out'], W['ln_g'].partition_broadcast(128), W['ln_b'].partition_broadcast(128),
                    D['G'], D['XR'], x_out, last)
    print("program instructions:", k.S.ninst, flush=True)
    return nc


SUF = 'abcd'


def layer_inputs(inputs, layers):
    cfa, cba = host_consts()
    rb = np.asarray(inputs['rel_bias'], np.float32)
    csa = np.zeros((128, sum(n for _, n in CS_LAYOUT)), np.float32)
    csa[:, 0] = LN_EPS
    csa[:, 1] = 1e-18
    csa[:, 2:18] = rb[31][None, :]
    csa[:, 18:34] = np.asarray(inputs['sinks_b'], np.float32)[None, :]
    csa[:, 34:290] = np.tile(np.asarray(inputs['fgate_bias_d'], np.float32), NTB)[None, :]
    m = {'cfa': cfa, 'cba': cba, 'csa': csa}
    names = [('w_in_a', 'w_out_a', 'ln_g_a', 'ln_b_a'), ('w_in_b', 'w_out_b', 'ln_g_b', 'ln_b_b'),
             ('w_in_c', 'w_out_c', 'ln_g_c', 'ln_b_c'), ('w_in_d', 'w_out_d', 'ln_g_d', 'ln_b_d')]
    for l in layers:
        wi, wo, lg, lb = names[l]
        m['w_in%d' % l] = np.ascontiguousarray(inputs[wi], np.float32)
        m['w_out%d' % l] = np.ascontiguousarray(inputs[wo], np.float32)
        m['ln_g%d' % l] = np.ascontiguousarray(inputs[lg], np.float32)
        m['ln_b%d' % l] = np.ascontiguousarray(inputs[lb], np.float32)
    if any(l in (0, 1) for l in layers):
        idx, msk = bias_gather_indices()
        m['btg'] = np.ascontiguousarray(np.transpose(rb[idx], (2, 0, 1)))
        m['bmask'] = msk
    if 0 in layers:
        m.update(nsa_host_inputs(inputs))
    return m


_PROG_CACHE = {}


def run_layers(x, inputs, layers, dbg=False, ncores=4):
    key = (tuple(layers), dbg)
    if key not in _PROG_CACHE:
        _PROG_CACHE[key] = build_program(layers, dbg)
    nc = _PROG_CACHE[key]
    shared = layer_inputs(inputs, layers)
    in_maps = []
    for b in range(ncores):
        mm_ = dict(shared)
        mm_['x'] = np.ascontiguousarray(x[b], np.float32)
        in_maps.append(mm_)
    res = run_bass_kernel_spmd(nc, in_maps, core_ids=list(range(ncores)), trace=bool(int(os.environ.get('DBG_TRACE', '0'))))
    if res.exec_time_ns is not None:
        print('exec_time_ns', res.exec_time_ns, flush=True)
    return res


FUSED = True


def kernel(**inputs):
    x = np.asarray(inputs['x'], np.float32)
    if FUSED:
        res = run_layers(x, inputs, [0, 1, 2, 3])
        return np.stack([res.results[b]['xo'] for b in range(4)], axis=0).astype(np.float32)
    for l in range(4):
        res = run_layers(x, inputs, [l])
        x = np.stack([res.results[b]['xo'] for b in range(4)], axis=0).astype(np.float32)
    return x
```

```python
import numpy as np
from contextlib import ExitStack
import concourse.bass as bass
import concourse.mybir as mybir
from concourse.bass_utils import run_bass_kernel_spmd

F32 = mybir.dt.float32
BF16 = mybir.dt.bfloat16
AF = mybir.ActivationFunctionType
ALU = mybir.AluOpType

SEQ = 2048
DM = 2048
NH = 16
HD = 128
NKV = 4
NTB = SEQ // 128
SCALE = HD ** -0.5
NEG = -30000.0
DN_ALPHA = 8 ** 0.25
LN_EPS = 1e-5
NBT = 896
import os
DBG_HEADS = int(os.environ.get('DBG_HEADS', '16'))
DBG_NOPREP = int(os.environ.get('DBG_NOPREP', '0'))

ENGS = ['pe', 'act', 'dve', 'pool', 'sp']
NSLOT = 8


class Sched:
    def __init__(self, nc, es):
        self.nc = nc
        self.esem = {e: es.enter_context(nc.semaphore("s_" + e)) for e in ENGS}
        self.ecnt = {e: 0 for e in ENGS}
        self.dq = ('sp', 'pool', 'act')
        self.dsem = {e: [es.enter_context(nc.semaphore("d_%s%d" % (e, i))) for i in range(NSLOT)]
                     for e in self.dq}
        self.dcnt = {e: [0] * NSLOT for e in self.dq}
        self.dnext = {e: 0 for e in self.dq}
        self.q = {e: [] for e in ENGS}
        self.waited = {e: {} for e in ENGS}
        self.bufs = {}
        self.ninst = 0

    def _deps(self, reads, writes, pe_chain):
        deps = []
        for k in reads:
            st = self.bufs.get(k)
            if st is not None and st[0] is not None:
                deps.append(st[0])
        for k in writes:
            st = self.bufs.get(k)
            if st is not None:
                if st[0] is not None and not (pe_chain and st[0][0] is self.esem['pe']):
                    deps.append(st[0])
                deps.extend(st[1])
        return deps

    def _filter(self, eng, deps):
        w = self.waited[eng]
        best = {}
        for sem, v in deps:
            key = id(sem)
            if w.get(key, 0) >= v:
                continue
            if key not in best or best[key][1] < v:
                best[key] = (sem, v)
        for key, (sem, v) in best.items():
            w[key] = v
        return list(best.values())

    def _record(self, tok, reads, writes):
        for k in reads:
            st = self.bufs.setdefault(k, [None, []])
            st[1].append(tok)
            if len(st[1]) > 12:
                d = {}
                for s_, v_ in st[1]:
                    if id(s_) not in d or d[id(s_)][1] < v_:
                        d[id(s_)] = (s_, v_)
                st[1] = list(d.values())
        for k in writes:
            self.bufs[k] = [tok, []]

    def op(self, eng, fn, reads=(), writes=(), pe_chain=False):
        deps = self._deps(reads, writes, pe_chain)
        self.ecnt[eng] += 1
        tok = (self.esem[eng], self.ecnt[eng])
        waits = self._filter(eng, deps)
        self.q[eng].append((waits, fn, tok, 1))
        self._record(tok, reads, writes)
        self.ninst += 1
        return tok

    def dma(self, eng, fn, reads=(), writes=()):
        deps = self._deps(reads, writes, False)
        s = self.dnext[eng]
        self.dnext[eng] = (s + 1) % NSLOT
        sem = self.dsem[eng][s]
        if self.dcnt[eng][s] > 0:
            deps.append((sem, 16 * self.dcnt[eng][s]))
        self.dcnt[eng][s] += 1
        tok = (sem, 16 * self.dcnt[eng][s])
        waits = self._filter(eng, deps)
        self.q[eng].append((waits, fn, tok, 16))
        self._record(tok, reads, writes)
        self.ninst += 1
        return tok

    def flush(self):
        nc = self.nc
        tails = {}
        for e in self.dq:
            tl = [(self.dsem[e][s], 16 * self.dcnt[e][s]) for s in range(NSLOT) if self.dcnt[e][s] > 0]
            tails[e] = self._filter(e, tl)
        q = self.q
        self.q = {e: [] for e in ENGS}

        def replay(eobj, lst, tail):
            for waits, fn, tok, inc in lst:
                for sem, v in waits:
                    eobj.wait_ge(sem, v)
                fn(eobj).then_inc(tok[0], inc)
            for sem, v in tail:
                eobj.wait_ge(sem, v)

        with nc.Block() as block:
            @block.tensor
            def _(e):
                replay(e, q['pe'], [])

            @block.scalar
            def _(e):
                replay(e, q['act'], tails['act'])

            @block.vector
            def _(e):
                replay(e, q['dve'], [])

            @block.gpsimd
            def _(e):
                replay(e, q['pool'], tails['pool'])

            @block.sync
            def _(e):
                replay(e, q['sp'], tails['sp'])
        self.bufs = {}


class Ring:
    def __init__(self, name, items):
        self.name = name
        self.items = items
        self.i = 0

    def next(self):
        i = self.i
        self.i = (i + 1) % len(self.items)
        it = self.items[i]
        return it, (self.name, it if isinstance(it, int) else i)


class K:
    def __init__(self, nc, es):
        self.nc = nc
        self.es = es
        self.S = Sched(nc, es)
        self.ps = [es.enter_context(nc.psum_tensor("ps%d" % i, [128, 512], F32)) for i in range(8)]

    def mm(self, out, lhsT, rhs, start, stop, reads, writes):
        return self.S.op('pe', lambda e: e.matmul(out, lhsT=lhsT, rhs=rhs, start=start, stop=stop),
                         reads, writes, pe_chain=True)

    def tr(self, out, in_, ident, reads, writes):
        return self.S.op('pe', lambda e: e.transpose(out, in_, ident), reads, writes, pe_chain=True)

    def act(self, out, in_, func, reads, writes, bias=None, scale=None):
        kw = {}
        if bias is not None:
            kw['bias'] = bias
        if scale is not None:
            kw['scale'] = scale
        return self.S.op('act', lambda e: e.activation(out=out, in_=in_, func=func, **kw), reads, writes)

    def tt(self, eng, out, in0, in1, op, reads, writes):
        return self.S.op(eng, lambda e: e.tensor_tensor(out=out, in0=in0, in1=in1, op=op), reads, writes)

    def ts(self, eng, out, in0, s1, op0, reads, writes, s2=None, op1=None):
        if op1 is None:
            return self.S.op(eng, lambda e: e.tensor_scalar(out=out, in0=in0, scalar1=s1, scalar2=None, op0=op0),
                             reads, writes)
        return self.S.op(eng, lambda e: e.tensor_scalar(out=out, in0=in0, scalar1=s1, scalar2=s2, op0=op0, op1=op1),
                         reads, writes)

    def stt(self, out, in0, scalar, in1, op0, op1, reads, writes):
        return self.S.op('dve', lambda e: e.scalar_tensor_tensor(out=out, in0=in0, scalar=scalar, in1=in1,
                                                                 op0=op0, op1=op1), reads, writes)

    def cp(self, eng, out, in_, reads, writes):
        if eng == 'act':
            return self.S.op('act', lambda e: e.activation(out=out, in_=in_, func=AF.Copy), reads, writes)
        return self.S.op(eng, lambda e: e.tensor_copy(out=out, in_=in_), reads, writes)

    def memset(self, eng, ap, val, writes):
        return self.S.op(eng, lambda e: e.memset(ap, val), (), writes)

    def dma(self, q, out, in_, reads, writes):
        return self.S.dma(q, lambda e: e.dma_start(out=out, in_=in_), reads, writes)


_SBN = [0]


def sbt(nc, ph, name, shape, dt):
    _SBN[0] += 1
    return ph.enter_context(nc.sbuf_tensor("%s_u%d" % (name, _SBN[0]), shape, dt))


def transpose_block(k, C, src, srckey, tb, trring):
    for q4 in range(4):
        bank, bkey = trring.next()
        for u in range(4):
            kc = q4 * 4 + u
            k.tr(k.ps[bank][:, u * 128:(u + 1) * 128], src[:, kc * 128:(kc + 1) * 128], C['ident_f'][:],
                 [srckey, 'const'], [bkey])
        k.cp('act', C['xT'][:, q4 * 4:q4 * 4 + 4, tb * 128:(tb + 1) * 128],
             k.ps[bank][:, :].rearrange("p (u f) -> p u f", u=4), [bkey], [('xT', tb, q4)])


def phase_x0(k, C, x_in, XR):
    nc = k.nc
    with ExitStack() as ph:
        xin = [sbt(nc, ph, "xin%d" % i, [128, DM], F32) for i in range(2)]
        trring = Ring('psb', [6, 7])
        for tb in range(NTB):
            b = tb % 2
            k.dma('sp', xin[b][:], x_in[tb * 128:(tb + 1) * 128, :], [], [('xin', b)])
            k.dma('sp', XR[tb * 128:(tb + 1) * 128, :], xin[b][:], [('xin', b)], [])
            transpose_block(k, C, xin[b], ('xin', b), tb, trring)
        k.S.flush()


def phase_P(k, C, W, groups):
    nc = k.nc
    Wv = W.rearrange("(kc p) c -> p kc c", p=128)
    xT = C['xT']
    with ExitStack() as ph:
        wst = [sbt(nc, ph, "wst%d" % i, [128, 8, 512], F32) for i in range(2)]
        wbf = [sbt(nc, ph, "wbf%d" % i, [128, 16, 512], BF16) for i in range(2)]
        sgb = Ring('sgb', [sbt(nc, ph, "sgb%d" % i, [128, SEQ], BF16) for i in range(3)])
        sgf = Ring('sgf', [sbt(nc, ph, "sgf%d" % i, [128, SEQ], F32) for i in range(2)])
        sgv = sbt(nc, ph, "sgv", [128, 4, NTB, 128], BF16)
        sfl = sbt(nc, ph, "sfl", [128, NTB, 16], F32)
        pring = Ring('psb', [0, 1, 2, 3])
        ev = [0]
        wsti = [0]

        def evac(out, in_, reads, writes):
            ev[0] += 1
            k.cp('act' if ev[0] % 2 else 'dve', out, in_, reads, writes)

        def load_group(gi):
            kind, c0, n, dst = groups[gi]
            wb = gi % 2
            for half in range(2):
                ws = wsti[0] % 2
                wsti[0] += 1
                k.dma('sp', wst[ws][:, :, :n], Wv[:, half * 8:(half + 1) * 8, c0:c0 + n], [], [('wst', ws)])
                k.cp('pool', wbf[wb][:, half * 8:(half + 1) * 8, :n], wst[ws][:, :, :n], [('wst', ws)],
                     [('wbf', wb, half)])

        load_group(0)
        for gi, (kind, c0, n, dst) in enumerate(groups):
            wb = gi % 2
            if gi + 1 < len(groups):
                load_group(gi + 1)
            wkeys = [('wbf', wb, 0), ('wbf', wb, 1)]
            if kind in ('FMB', 'FMF'):
                nct = (n + 127) // 128
                for ct in range(nct):
                    m_ = min(128, n - ct * 128)
                    stg, skey = (sgb if kind == 'FMB' else sgf).next()
                    for tc in range(4):
                        bank, bkey = pring.next()
                        for kc in range(16):
                            k.mm(k.ps[bank][:m_, :], wbf[wb][:, kc, ct * 128:ct * 128 + m_],
                                 xT[:, kc, tc * 512:(tc + 1) * 512], kc == 0, kc == 15,
                                 [wkeys[kc // 8]], [bkey])
                        evac(stg[:m_, tc * 512:(tc + 1) * 512], k.ps[bank][:m_, :], [bkey], [skey])
                    k.dma('sp', dst[ct], stg[:m_, :], [skey], [])
            elif kind == 'TMV':
                ns = n // 128
                for tb in range(NTB):
                    bank, bkey = pring.next()
                    for kc in range(16):
                        k.mm(k.ps[bank][:, :n], xT[:, kc, tb * 128:(tb + 1) * 128], wbf[wb][:, kc, :n],
                             kc == 0, kc == 15, [wkeys[kc // 8]], [bkey])
                    evac(sgv[:, 0:ns, tb, :], k.ps[bank][:, :n].rearrange("p (s d) -> p s d", d=128), [bkey],
                         ['sgv'])
                for si in range(ns):
                    k.dma('sp', dst[si], sgv[:, si, :, :], ['sgv'], [])
            else:
                for tb in range(NTB):
                    bank, bkey = pring.next()
                    for kc in range(16):
                        k.mm(k.ps[bank][:, :n], xT[:, kc, tb * 128:(tb + 1) * 128], wbf[wb][:, kc, :n],
                             kc == 0, kc == 15, [wkeys[kc // 8]], [bkey])
                    evac(sfl[:, tb, :n], k.ps[bank][:, :n], [bkey], ['sfl'])
                k.dma('sp', dst, sfl[:, :, :n], ['sfl'], [])
        k.S.flush()


def phase_B(k, C, Wout, lng_b, lnb_b, G, XR, xout, last):
    nc = k.nc
    Wv = Wout.rearrange("(h p) c -> p h c", p=128)
    with ExitStack() as ph:
        wo = sbt(nc, ph, "wo", [128, NH, DM], BF16)
        wst = [sbt(nc, ph, "wost%d" % i, [128, DM], F32) for i in range(2)]
        gbc = sbt(nc, ph, "gbc", [128, DM], F32)
        bbc = sbt(nc, ph, "bbc", [128, DM], F32)
        gt = [sbt(nc, ph, "gt%d" % i, [128, NH, 128], BF16) for i in range(2)]
        xr = [sbt(nc, ph, "xr%d" % i, [128, DM], F32) for i in range(2)]
        st6 = sbt(nc, ph, "st6", [128, 4, 6], F32)
        mv = sbt(nc, ph, "mv", [128, 2], F32)
        rstd = sbt(nc, ph, "rstd", [128, 1], F32)
        k.dma('sp', gbc[:], lng_b, [], ['gbc'])
        k.dma('sp', bbc[:], lnb_b, [], ['bbc'])
        Gv = G.rearrange("h p s -> p h s")

        def load_tb(tb):
            b = tb % 2
            k.dma('sp', gt[b][:], Gv[:, :, tb * 128:(tb + 1) * 128], [], [('gt', b)])
            k.dma('sp', xr[b][:], XR[tb * 128:(tb + 1) * 128, :], [], [('xr', b, cb) for cb in range(4)])

        load_tb(0)
        for i in range(NH):
            ws = i % 2
            k.dma('sp', wst[ws][:], Wv[:, i, :], [], [('wost', ws)])
            k.cp('pool', wo[:, i, :], wst[ws][:], [('wost', ws)], [('wo', i)])
        trring = Ring('psb', [4, 5])
        for tb in range(NTB):
            b = tb % 2
            if tb + 1 < NTB:
                load_tb(tb + 1)
            for cb in range(4):
                for h in range(NH):
                    k.mm(k.ps[cb][:, :], gt[b][:, h, :], wo[:, h, cb * 512:(cb + 1) * 512], h == 0, h == NH - 1,
                         [('gt', b), ('wo', h)], [('psb', cb)])
                xs = xr[b][:, cb * 512:(cb + 1) * 512]
                k.stt(xs, xs, DN_ALPHA, k.ps[cb][:, :], ALU.mult, ALU.add, [('xr', b, cb), ('psb', cb)],
                      [('xr', b, cb)])
                k.S.op('dve', (lambda o, i_: (lambda e: e.bn_stats(out=o, in_=i_)))(st6[:, cb, :], xs),
                       [('xr', b, cb)], [('st6', cb)])
            k.S.op('dve', lambda e: e.bn_aggr(out=mv[:], in_=st6[:].rearrange("p a b -> p (a b)")),
                   [('st6', cb) for cb in range(4)], ['mv'])
            k.act(rstd[:], mv[:, 1:2], AF.Ln, ['mv'], ['rstd'], bias=C['epscol'][:], scale=1.0)
            k.act(rstd[:], rstd[:], AF.Exp, ['rstd'], ['rstd'], scale=-0.5)
            rk = [('xr', b, cb) for cb in range(4)]
            k.ts('dve', xr[b][:], xr[b][:], mv[:, 0:1], ALU.subtract, rk + ['mv', 'rstd'], rk, s2=rstd[:, 0:1],
                 op1=ALU.mult)
            k.tt('pool', xr[b][:], xr[b][:], gbc[:], ALU.mult, rk + ['gbc'], rk)
            k.tt('pool', xr[b][:], xr[b][:], bbc[:], ALU.add, rk + ['bbc'], rk)
            if last:
                k.dma('sp', xout[tb * 128:(tb + 1) * 128, :], xr[b][:], rk, [])
            else:
                k.dma('sp', XR[tb * 128:(tb + 1) * 128, :], xr[b][:], rk, [])
                for q4 in range(4):
                    bank, bkey = trring.next()
                    for u in range(4):
                        kc = q4 * 4 + u
                        k.tr(k.ps[bank][:, u * 128:(u + 1) * 128], xr[b][:, kc * 128:(kc + 1) * 128],
                             C['ident_f'][:], rk + ['const'], [bkey])
                    k.cp('act', C['xT'][:, q4 * 4:q4 * 4 + 4, tb * 128:(tb + 1) * 128],
                         k.ps[bank][:, :].rearrange("p (u f) -> p u f", u=4), [bkey], [('xT', tb, q4)])
        k.S.flush()


class ACtx:
    def __init__(self, k, C, ph, n_pt=4):
        nc = k.nc
        self.k = k
        self.C = C
        self.sc = Ring('psb', [0, 1, 2])
        self.pt = Ring('pt', [sbt(nc, ph, "pt%d" % i, [128, 512], BF16) for i in range(n_pt)])
        self.ob = Ring('ob', [(3, 4), (5, 6)])
        self.ft = Ring('ft', [sbt(nc, ph, "ft%d" % i, [128, 512], F32) for i in range(4)])
        self.gst = Ring('gst', [sbt(nc, ph, "gst%d" % i, [128, SEQ], BF16) for i in range(2)])

    def zero_init(self, banks, cols=512):
        k, C = self.k, self.C
        for b in banks:
            k.mm(k.ps[b][:, 0:cols], C['zeros_bf'][:, 0:128], C['zeros_bf'][:, 0:cols], True, False,
                 ['const'], [('psb', b)])


def softmax_tiles(A, qT, qkey, m, tiles, obank, dbank, fox_bias=None):
    k, C = A.k, A.C
    pend = None
    n = len(tiles)

    def pv(p):
        t, pt, pkey, last = p
        nk, lo, hi = t['nk'], t['lo'], t['hi']
        k.mm(k.ps[obank][:, lo:hi], t['v'], pt[:nk, lo:hi], False, last, [t['vkey'], pkey], [('psb', obank)])
        if dbank is not None:
            k.mm(k.ps[dbank][:, lo:hi], C['ones_bf'][:nk, :], pt[:nk, lo:hi], False, last, ['const', pkey],
                 [('psb', dbank)])

    pendq = []
    for ti, t in enumerate(tiles):
        bank, bkey = A.sc.next()
        nk, lo, hi = t['nk'], t['lo'], t['hi']
        adds = t.get('adds', [])
        k.mm(k.ps[bank][:nk, lo:hi], t['kT'], qT[:, m * 512 + lo:m * 512 + hi], True, len(adds) == 0,
             [t['kkey'], qkey], [bkey])
        for ai, (l_, r_, rows, clo, chi, keys) in enumerate(adds):
            k.mm(k.ps[bank][:rows, clo:chi], l_, r_, False, ai == len(adds) - 1, keys, [bkey])
        if len(pendq) >= 2:
            pv(pendq.pop(0))
        pt, pkey = A.pt.next()
        if fox_bias is not None:
            for c in range(lo // 128, hi // 128):
                fbias, fkey = fox_bias(t['j'], m * 4 + c)
                k.act(pt[:nk, c * 128:(c + 1) * 128], k.ps[bank][:nk, c * 128:(c + 1) * 128], AF.Exp, [bkey, fkey],
                      [pkey], bias=fbias, scale=SCALE)
        else:
            k.act(pt[:nk, lo:hi], k.ps[bank][:nk, lo:hi], AF.Exp, [bkey], [pkey], scale=SCALE)
        pendq.append((t, pt, pkey, ti == n - 1))
    for p in pendq:
        pv(p)


def silu_parts(A, zt, zkey, m):
    k = A.k
    e1, e1k = A.ft.next()
    k.act(e1[:, :], zt[:, m * 512:(m + 1) * 512], AF.Exp, [zkey], [e1k], scale=-1.0)
    k.act(e1[:, :], e1[:, :], AF.Ln, [e1k], [e1k], bias=1.0, scale=1.0)
    return e1, e1k


def epilogue_single(A, zt, zkey, m, obank, dbank, gst, gkey, den_bias):
    k, C = A.k, A.C
    l1, l1k = silu_parts(A, zt, zkey, m)
    if dbank is not None:
        l2, l2k = A.ft.next()
        k.act(l2[:, :], k.ps[dbank][:, :], AF.Ln, [('psb', dbank), 'const', 'sinkcol'], [l2k], bias=den_bias, scale=1.0)
        k.tt('pool', l1[:, :], l1[:, :], l2[:, :], ALU.add, [l1k, l2k], [l1k])
    k.act(l1[:, :], l1[:, :], AF.Exp, [l1k], [l1k], scale=-1.0)
    t1, t1k = A.ft.next()
    k.tt('dve', t1[:, :], k.ps[obank][:, :], l1[:, :], ALU.mult, [('psb', obank), l1k], [t1k])
    k.tt('dve', gst[:, m * 512:(m + 1) * 512], t1[:, :], zt[:, m * 512:(m + 1) * 512], ALU.mult, [t1k, zkey],
         [(gkey, m)])


def load_head(k, bufs, keyname, idx, srcs):
    for t, d in srcs:
        k.dma('sp', t, d, [], [(keyname, idx)])


def phase_A_swa(k, C, D):
    nc = k.nc
    with ExitStack() as ph:
        A = ACtx(k, C, ph)
        qt = [sbt(nc, ph, "qt%d" % i, [128, SEQ], BF16) for i in range(2)]
        kt = [sbt(nc, ph, "kt%d" % i, [128, SEQ], BF16) for i in range(2)]
        vt = [sbt(nc, ph, "vt%d" % i, [128, NTB, 128], BF16) for i in range(2)]
        zt = [sbt(nc, ph, "zt%d" % i, [128, SEQ], F32) for i in range(2)]
        bt = [sbt(nc, ph, "bt%d" % i, [128, 2, NBT], BF16) for i in range(2)]
        sk = sbt(nc, ph, "sk", [128, NH], F32)
        k.tt('dve', sk[:], C['sinkb'][:], C['rb31'][:], ALU.subtract, ['const'], ['sinkcol'])
        k.act(sk[:], sk[:], AF.Exp, ['sinkcol'], ['sinkcol'])
        def loads(h):
            g = h // 4
            hb = h % 2
            gb = g % 2
            if h % 4 == 0:
                k.dma('sp', kt[gb][:], D['FMB'][16 + g], [], [('kt', gb)])
                k.dma('sp', vt[gb][:], D['VS'][g], [], [('vt', gb)])
            k.dma('sp', qt[hb][:], D['FMB'][h], [], [('qt', hb)])
            k.dma('sp', zt[hb][:], D['ZT'][h], [], [('zt', hb)])
            k.dma('sp', bt[hb][:], D['BTH'][h], [], [('bt', hb)])

        loads(0)
        for h in range(NH):
            g = h // 4
            hb = h % 2
            gb = g % 2
            if h + 1 < NH:
                loads(h + 1)
            gst, gkey = A.gst.next()
            ident = C['ident_bf']
            for m in range(4):
                (ob, db), _ = A.ob.next()
                A.zero_init([ob, db])
                tiles = []
                for j in range(max(0, 4 * m - 1), 4 * m + 4):
                    r = j - 4 * m
                    adds = []
                    if r == -1:
                        lo, hi = 0, 128
                        for hl in range(2):
                            adds.append((ident[:], bt[hb][:, hl, 256:384], 128, 0, 128, ['const', ('bt', hb)]))
                    else:
                        lo, hi = 128 * r, min(128 * (r + 2), 512)
                        for hl in range(2):
                            adds.append((ident[:], bt[hb][:, hl, 0:128], 128, lo, lo + 128, ['const', ('bt', hb)]))
                        if r < 3:
                            for hl in range(2):
                                adds.append((ident[:], bt[hb][:, hl, 256:384], 128, lo + 128, lo + 256,
                                             ['const', ('bt', hb)]))
                    tiles.append(dict(kT=kt[gb][:, j * 128:(j + 1) * 128], kkey=('kt', gb), nk=128, lo=lo, hi=hi,
                                      adds=adds, v=vt[gb][:, j, :], vkey=('vt', gb), j=j))
                softmax_tiles(A, qt[hb], ('qt', hb), m, tiles, ob, db)
                epilogue_single(A, zt[hb], ('zt', hb), m, ob, db, gst, gkey, sk[:, h:h + 1])
            k.dma('sp', D['G'][h], gst[:], [(gkey, m) for m in range(4)], [])
        k.S.flush()


def phase_A_fox(k, C, D):
    nc = k.nc
    with ExitStack() as ph:
        A = ACtx(k, C, ph)
        qt = [sbt(nc, ph, "qt%d" % i, [128, SEQ], BF16) for i in range(2)]
        kt = [sbt(nc, ph, "kt%d" % i, [128, SEQ], BF16) for i in range(2)]
        vt = [sbt(nc, ph, "vt%d" % i, [128, NTB, 128], BF16) for i in range(2)]
        zt = [sbt(nc, ph, "zt%d" % i, [128, SEQ], F32) for i in range(2)]
        fl = sbt(nc, ph, "fl", [128, NTB, NH], F32)
        pfx = sbt(nc, ph, "pfx", [128, NTB, NH], F32)
        cs = sbt(nc, ph, "cs", [128, NTB, NH], F32)
        csm = sbt(nc, ph, "csm", [128, NTB, NH], F32)
        bm = [sbt(nc, ph, "bm%d" % i, [128, NTB, NTB], F32) for i in range(2)]
        k.dma('sp', fl[:], D['FL'], [], ['fl'])
        fl2 = fl[:].rearrange("p a b -> p (a b)")
        k.tt('dve', fl2, fl2, C['fbb'][:], ALU.add, ['fl', 'const'], ['fl'])
        k.act(fl2, fl2, AF.Exp, ['fl'], ['fl'], scale=-1.0)
        k.act(fl2, fl2, AF.Ln, ['fl'], ['fl'], bias=1.0, scale=1.0)
        k.memset('dve', pfx[:, 0, :], 0.0, [('pfx', 0)])
        for tb in range(1, NTB):
            k.tt('dve', pfx[:, tb, :], pfx[:, tb - 1, :], fl[:, tb - 1, :], ALU.add, [('pfx', tb - 1), 'fl'],
                 [('pfx', tb)])
        pk = [('pfx', tb) for tb in range(NTB)]
        k.mm(k.ps[7][:, 0:256], C['tri_f'][:], fl2, True, False, ['const', 'fl'], [('psb', 7)])
        k.mm(k.ps[7][:, 0:256], C['ones_f'][:], pfx[:].rearrange("p a b -> p (a b)"), False, True, ['const'] + pk,
             [('psb', 7)])
        k.cp('dve', cs[:].rearrange("p a b -> p (a b)"), k.ps[7][:, 0:256], [('psb', 7)], ['cs'])
        k.mm(k.ps[7][:, 256:512], C['sel64_f'][:], cs[:].rearrange("p a b -> p (a b)"), True, True, ['const', 'cs'],
             [('psb', 7)])
        k.cp('dve', csm[:].rearrange("p a b -> p (a b)"), k.ps[7][:, 256:512], [('psb', 7)], ['csm'])
        def loads(h):
            hb = h % 2
            k.dma('sp', kt[hb][:], D['FMB'][16 + h], [], [('kt', hb)])
            k.dma('sp', vt[hb][:], D['VS'][h], [], [('vt', hb)])
            k.dma('sp', qt[hb][:], D['FMB'][h], [], [('qt', hb)])
            k.dma('sp', zt[hb][:], D['ZT'][h], [], [('zt', hb)])

        loads(0)
        for h in range(DBG_HEADS):
            hb = h % 2
            if h + 1 < NH:
                loads(h + 1)
            for i in range(NTB):
                k.ts('dve', bm[hb][:, i, :], cs[:, :, h], csm[:, i, h:h + 1], ALU.subtract, ['cs', 'csm'],
                     [('bm', hb)])
            gst, gkey = A.gst.next()
            fb = (lambda hb_: (lambda j, i: (bm[hb_][:, i, j:j + 1], ('bm', hb_))))(hb)
            for m in range(4):
                (ob, db), _ = A.ob.next()
                A.zero_init([ob, db])
                tiles = []
                for j in range(0, 4 * m + 4):
                    r = j - 4 * m
                    adds = []
                    lo, hi = (0, 512) if r < 0 else (128 * r, 512)
                    if r >= 0:
                        adds.append((C['ident_bf'][:], C['mcausal'][:], 128, lo, lo + 128, ['const']))
                    tiles.append(dict(kT=kt[hb][:, j * 128:(j + 1) * 128], kkey=('kt', hb), nk=128, lo=lo, hi=hi,
                                      adds=adds, v=vt[hb][:, j, :], vkey=('vt', hb), j=j))
                softmax_tiles(A, qt[hb], ('qt', hb), m, tiles, ob, db, fox_bias=fb)
                epilogue_single(A, zt[hb], ('zt', hb), m, ob, db, gst, gkey, C['tinycol'][:])
            k.dma('sp', D['G'][h], gst[:], [(gkey, m) for m in range(4)], [])
            if h % 4 == 3 and h + 1 < NH:
                k.S.flush()
        k.S.flush()


def phase_A_sb(k, C, D):
    nc = k.nc
    with ExitStack() as ph:
        A = ACtx(k, C, ph, n_pt=3)
        qt = [sbt(nc, ph, "qt%d" % i, [128, SEQ], BF16) for i in range(2)]
        kt = [sbt(nc, ph, "kt%d" % i, [128, SEQ], BF16) for i in range(2)]
        vt = [sbt(nc, ph, "vt%d" % i, [128, NTB, 128], BF16) for i in range(2)]
        zt = [sbt(nc, ph, "zt%d" % i, [128, SEQ], F32) for i in range(2)]
        spr = Ring('spt', [sbt(nc, ph, "spt%d" % i, [128, 512], F32) for i in range(3)])
        e1r = Ring('e1t', [sbt(nc, ph, "e1t%d" % i, [128, 512], F32) for i in range(3)])
        spacc = [sbt(nc, ph, "spacc%d" % i, [128, 512], F32) for i in range(2)]
        zring = Ring('psb', [0, 1])
        aring = Ring('psb', [2, 7])
        obanks = [3, 5]
        cnt = 0
        def loads(h):
            hb = h % 2
            k.dma('sp', kt[hb][:], D['FMB'][16 + h], [], [('kt', hb)])
            k.dma('sp', vt[hb][:], D['VS'][h], [], [('vt', hb)])
            k.dma('sp', qt[hb][:], D['FMB'][h], [], [('qt', hb)])
            k.dma('sp', zt[hb][:], D['ZT'][h], [], [('zt', hb)])

        loads(0)
        for h in range(NH):
            hb = h % 2
            if h + 1 < NH:
                loads(h + 1)
            gst, gkey = A.gst.next()
            for m in range(4):
                ob = obanks[cnt % 2]
                sa = spacc[cnt % 2]
                sakey = ('spacc', cnt % 2)
                cnt += 1
                A.zero_init([ob])
                k.memset('pool', sa[:, :], 0.0, [sakey])
                js = list(range(4 * m + 3, -1, -1))
                n = len(js)
                st = [None] * n

                def stage1(i):
                    j = js[i]
                    r = j - 4 * m
                    lo = 0 if r < 0 else 128 * r
                    zb_, zkey_ = zring.next()
                    zb = zb_
                    zkey = ('psb', zb)
                    k.mm(k.ps[zb][:, lo:512], kt[hb][:, j * 128:(j + 1) * 128], qt[hb][:, m * 512 + lo:(m + 1) * 512],
                         True, True, [('kt', hb), ('qt', hb)], [zkey])
                    sp, spk = spr.next()
                    k.act(sp[:, lo:512], k.ps[zb][:, lo:512], AF.Exp, [zkey], [spk], scale=SCALE)
                    k.act(sp[:, lo:512], sp[:, lo:512], AF.Ln, [spk], [spk], bias=1.0, scale=1.0)
                    if r >= 0:
                        k.tt('pool', sp[:, lo:lo + 128], sp[:, lo:lo + 128], C['mstrict_f'][:], ALU.mult,
                             [spk, 'const'], [spk])
                    st[i] = dict(j=j, r=r, lo=lo, zb=zb, zkey=zkey, sp=sp, spk=spk)

                def stage2(i):
                    s = st[i]
                    lo, sp, spk = s['lo'], s['sp'], s['spk']
                    ab_, _ = aring.next()
                    ab = ab_
                    akey = ('psb', ab)
                    first = (i == 0)
                    diag = s['r'] >= 0
                    k.mm(k.ps[ab][:, lo:512], C['ustrict_f'][:], sp[:, lo:512], True, first and not diag,
                         ['const', spk], [akey])
                    if not first:
                        k.mm(k.ps[ab][:, lo:512], C['ones_f'][:], sa[:, lo:512], False, not diag, ['const', sakey],
                             [akey])
                    if diag:
                        k.mm(k.ps[ab][:, lo:lo + 128], C['ident_bf'][:], C['mpos_bf'][:], False, True, ['const'],
                             [akey])
                    if i < n - 1:
                        k.tt('pool', sa[:, lo:512], sa[:, lo:512], sp[:, lo:512], ALU.add, [sakey, spk], [sakey])
                    e1, e1k = e1r.next()
                    k.stt(e1[:, lo:512], k.ps[s['zb']][:, lo:512], SCALE, sp[:, lo:512], ALU.mult, ALU.subtract,
                          [s['zkey'], spk], [e1k])
                    k.tt('dve', e1[:, lo:512], e1[:, lo:512], k.ps[ab][:, lo:512], ALU.subtract, [e1k, akey], [e1k])
                    pt, pkey = A.pt.next()
                    k.act(pt[:, lo:512], e1[:, lo:512], AF.Exp, [e1k], [pkey])
                    s['pt'] = pt
                    s['pkey'] = pkey

                def stage3(i):
                    s = st[i]
                    lo = s['lo']
                    k.mm(k.ps[ob][:, lo:512], vt[hb][:, s['j'], :], s['pt'][:, lo:512], False, i == n - 1,
                         [('vt', hb), s['pkey']], [('psb', ob)])

                for step in range(n + 2):
                    if step < n:
                        stage1(step)
                    if 0 <= step - 1 < n:
                        stage2(step - 1)
                    if 0 <= step - 2 < n:
                        stage3(step - 2)
                epilogue_single(A, zt[hb], ('zt', hb), m, ob, None, gst, gkey, None)
            k.dma('sp', D['G'][h], gst[:], [(gkey, m) for m in range(4)], [])
            if h % 4 == 3 and h + 1 < NH:
                k.S.flush()
        k.S.flush()


NSB_LAYOUT = [('shiftm', 512), ('blksel', 2048), ('interA', 64)]


def nsa_inputs(nc, dt_in):
    return dict(
        w1k=dt_in("w1k", [128, 32, 128]).ap(), w1v=dt_in("w1v", [128, 32, 128]).ap(),
        w2k=dt_in("w2k", [128, 128]).ap(), w2v=dt_in("w2v", [128, 128]).ap(),
        pek=dt_in("pek", [128, 32]).ap(), pev=dt_in("pev", [128, 32]).ap(),
        nsb=dt_in("nsb", [128, sum(n for _, n in NSB_LAYOUT)]).ap(),
        mulc=dt_in("mulc", [128, 512]).ap(), addc=dt_in("addc", [128, 512]).ap(),
    )


def nsa_host_inputs(inputs):
    f32 = np.float32
    m = {}
    m['w1k'] = np.ascontiguousarray(np.transpose(np.asarray(inputs['cmp_w1_k'], f32), (1, 0, 2)))
    m['w1v'] = np.ascontiguousarray(np.transpose(np.asarray(inputs['cmp_w1_v'], f32), (1, 0, 2)))
    m['w2k'] = np.ascontiguousarray(inputs['cmp_w2_k'], f32)
    m['w2v'] = np.ascontiguousarray(inputs['cmp_w2_v'], f32)
    m['pek'] = np.ascontiguousarray(np.asarray(inputs['cmp_pe_k'], f32).T)
    m['pev'] = np.ascontiguousarray(np.asarray(inputs['cmp_pe_v'], f32).T)
    shiftm = np.zeros((128, 512), f32)
    for mm_ in range(4):
        for kk in range(40):
            n = 32 * mm_ - 9 + kk
            if 0 <= n < 127:
                shiftm[kk, mm_ * 128 + n] = 1.0
    blksel = np.zeros((128, 2048), f32)
    for j in range(16):
        for s_ in range(128):
            blksel[2 * j + (1 if s_ >= 64 else 0), j * 128 + s_] = 1.0
    n_ = np.arange(127)
    jj = np.arange(32)
    inter = np.clip(np.minimum(n_[:, None] * 16 + 32, jj[None, :] * 64 + 64)
                    - np.maximum(n_[:, None] * 16, jj[None, :] * 64), 0, None) / 32.0
    interA = np.zeros((128, 64), f32)
    interA[:127, :32] = inter
    interA[:127, 32] = 1.0
    m['nsb'] = np.ascontiguousarray(np.concatenate([shiftm, blksel, interA], axis=1))
    t = np.arange(SEQ)
    allowed = (jj[None, :] * 64 <= t[:, None])
    cur = t // 64
    forced = (jj[None, :] == 0) | (jj[None, :] == cur[:, None]) | (jj[None, :] == cur[:, None] - 1)
    mulc = (allowed & ~forced).astype(f32)
    addc = np.where(forced & allowed, 1e4, np.where(allowed, 0.0, -1e30)).astype(f32)
    m['mulc'] = np.ascontiguousarray(mulc.reshape(NTB, 128, 32).transpose(1, 0, 2).reshape(128, 512))
    m['addc'] = np.ascontiguousarray(addc.reshape(NTB, 128, 32).transpose(1, 0, 2).reshape(128, 512))
    return m


def phase_A_nsa(k, C, D, NS):
    nc = k.nc
    FMB, VS, ZT, GL, BTH, G = D['FMB'], D['VS'], D['ZT'], D['GL'], D['BTH'], D['G']
    with ExitStack() as ph:
        A = ACtx(k, C, ph)
        w1 = [sbt(nc, ph, "w1_%d" % i, [128, 32, 128], BF16) for i in range(2)]
        w2 = [sbt(nc, ph, "w2_%d" % i, [128, 128], BF16) for i in range(2)]
        pe = [sbt(nc, ph, "pe_%d" % i, [128, 32], BF16) for i in range(2)]
        bcol = sbt(nc, ph, "bcol", [128, 4], F32)
        nsb = sbt(nc, ph, "nsb", [128, sum(n for _, n in NSB_LAYOUT)], BF16)
        mulc = sbt(nc, ph, "mulc", [128, 512], F32)
        addc = sbt(nc, ph, "addc", [128, 512], F32)
        shiftm = nsb[:, 0:512]
        blksel = nsb[:, 512:2560]
        interA = nsb[:, 2560:2624]
        glt = sbt(nc, ph, "glt", [128, SEQ], F32)
        ct = [sbt(nc, ph, "ct%d" % i, [128, SEQ], BF16) for i in range(2)]
        kst = sbt(nc, ph, "kst", [128, SEQ], BF16)
        kwt = sbt(nc, ph, "kwt", [128, SEQ], BF16)
        vst = sbt(nc, ph, "vst", [128, NTB, 128], BF16)
        vwt = sbt(nc, ph, "vwt", [128, NTB, 128], BF16)
        hid = [sbt(nc, ph, "hid%d" % i, [128, 128], BF16) for i in range(2)]
        kcmp = sbt(nc, ph, "kcmp", [128, 128], BF16)
        vcmp = sbt(nc, ph, "vcmp", [128, 128], BF16)
        qt = [sbt(nc, ph, "qt%d" % i, [128, SEQ], BF16) for i in range(2)]
        zt = [sbt(nc, ph, "zt%d" % i, [128, SEQ], F32) for i in range(2)]
        bt = [sbt(nc, ph, "bt%d" % i, [128, 2, NBT], BF16) for i in range(2)]
        gat = [sbt(nc, ph, "gat%d" % i, [128, 3, 512], F32) for i in range(2)]
        acc = [sbt(nc, ph, "acc%d" % i, [128, 512], F32) for i in range(2)]
        impa = sbt(nc, ph, "impa", [128, NTB, 32], F32)
        rec = sbt(nc, ph, "rec", [128, 4], F32)
        top8 = sbt(nc, ph, "top8", [128, 8], F32)
        nsl = sbt(nc, ph, "nsl", [128, NTB, 32], F32)
        nselT = sbt(nc, ph, "nselT", [32, SEQ], BF16)
        ident = C['ident_bf']

        k.dma('pool', w1[0][:], NS['w1k'], [], ['w1'])
        k.dma('pool', w1[1][:], NS['w1v'], [], ['w1'])
        k.dma('pool', w2[0][:], NS['w2k'], [], ['w1'])
        k.dma('pool', w2[1][:], NS['w2v'], [], ['w1'])
        k.dma('pool', pe[0][:], NS['pek'], [], ['w1'])
        k.dma('pool', pe[1][:], NS['pev'], [], ['w1'])
        k.dma('pool', nsb[:], NS['nsb'], [], ['nsb'])
        k.dma('sp', mulc[:], NS['mulc'], [], ['mulc'])
        k.dma('sp', addc[:], NS['addc'], [], ['mulc'])
        for i in range(2):
            for l in range(32):
                k.mm(k.ps[7][:, i:i + 1], w1[i][:, l, :], pe[i][:, l:l + 1], l == 0, l == 31, ['w1'], [('psb', 7)])
        k.cp('dve', bcol[:, 0:1], k.ps[7][:, 0:1], [('psb', 7)], ['bcol'])
        k.cp('dve', bcol[:, 2:3], k.ps[7][:, 1:2], [('psb', 7)], ['bcol'])
        k.ts('dve', bcol[:, 1:2], k.ps[7][:, 0:1], -1.0, ALU.mult, [('psb', 7), 'bcol'], ['bcol'])
        k.ts('dve', bcol[:, 3:4], k.ps[7][:, 1:2], -1.0, ALU.mult, [('psb', 7), 'bcol'], ['bcol'])
        k.dma('sp', glt[:48, :], GL[0:48], [], ['glt'])
        k.act(glt[:48, :], glt[:48, :], AF.Exp, ['glt'], ['glt'], scale=-1.0)
        k.act(glt[:48, :], glt[:48, :], AF.Ln, ['glt'], ['glt'], bias=1.0, scale=1.0)
        k.act(glt[:48, :], glt[:48, :], AF.Exp, ['glt'], ['glt'], scale=-1.0)
        k.dma('sp', GL[0:48], glt[:48, :], ['glt'], [])
        k.S.flush()

        def cmp_tile(hb, m):
            nk = min(127, 32 * m + 31)
            adds = []
            for hl in range(2):
                adds.append((shiftm[:40, m * 128:m * 128 + nk], bt[hb][:40, hl, 384:896], nk, 0, 512,
                             ['nsb', ('bt', hb)]))
            return dict(kT=kcmp[:, :nk], kkey='kcmp', nk=nk, lo=0, hi=512, adds=adds, v=vcmp[:nk, :], vkey='vcmp',
                        j=0)

        for g in range(NKV):
            k.dma('sp', ct[0][:], FMB[16 + g], [], [('ct', 0)])
            k.dma('sp', ct[1][:], FMB[20 + g], [], [('ct', 1)])
            k.dma('sp', kst[:], FMB[24 + g], [], ['kst'])
            k.dma('sp', kwt[:], FMB[28 + g], [], ['kwt'])
            k.dma('sp', vst[:], VS[g], [], ['vst'])
            k.dma('sp', vwt[:], VS[4 + g], [], ['vwt'])
            for i in range(2):
                for l in range(32):
                    k.mm(k.ps[7][:, 0:127], w1[i][:, l, :], ct[i][:, l:l + 16 * 126 + 1:16], l == 0, l == 31,
                         ['w1', ('ct', i)], [('psb', 7)])
                f1, f1k = A.ft.next()
                k.act(f1[:, 0:127], k.ps[7][:, 0:127], AF.Exp, [('psb', 7), 'bcol'], [f1k],
                      bias=bcol[:, 2 * i + 1:2 * i + 2], scale=-1.0)
                k.act(f1[:, 0:127], f1[:, 0:127], AF.Ln, [f1k], [f1k], bias=1.0, scale=1.0)
                k.act(f1[:, 0:127], f1[:, 0:127], AF.Exp, [f1k], [f1k], scale=-1.0)
                k.stt(hid[i][:, 0:127], k.ps[7][:, 0:127], bcol[:, 2 * i:2 * i + 1], f1[:, 0:127], ALU.add, ALU.mult,
                      [('psb', 7), 'bcol', f1k], [('hid', i)])
                if i == 0:
                    k.mm(k.ps[7][:, 128:255], w2[0][:], hid[0][:, 0:127], True, True, ['w1', ('hid', 0)],
                         [('psb', 7)])
                    k.cp('dve', kcmp[:, 0:127], k.ps[7][:, 128:255], [('psb', 7)], ['kcmp'])
                else:
                    k.mm(k.ps[7][:127, 256:384], hid[1][:, 0:127], w2[1][:], True, True, ['w1', ('hid', 1)],
                         [('psb', 7)])
                    k.cp('dve', vcmp[:127, :], k.ps[7][:127, 256:384], [('psb', 7)], ['vcmp'])
            for hi_ in range(4):
                h = 4 * g + hi_
                hb = h % 2
                k.dma('sp', qt[hb][:], FMB[h], [], [('qt', hb)])
                k.dma('sp', bt[hb][:], BTH[h], [], [('bt', hb)])
                for m in range(4):
                    t = cmp_tile(hb, m)
                    nk = t['nk']
                    bank, bkey = A.sc.next()
                    k.mm(k.ps[bank][:nk, 0:512], t['kT'], qt[hb][:, m * 512:(m + 1) * 512], True, False,
                         ['kcmp', ('qt', hb)], [bkey])
                    for ai, (l_, r_, rows, clo, chi, keys) in enumerate(t['adds']):
                        k.mm(k.ps[bank][:rows, clo:chi], l_, r_, False, ai == 1, keys, [bkey])
                    pt, pkey = A.pt.next()
                    k.act(pt[:nk, 0:512], k.ps[bank][:nk, 0:512], AF.Exp, [bkey], [pkey], scale=SCALE)
                    for c in range(4):
                        k.mm(k.ps[7][:, c * 33:c * 33 + 33], pt[:nk, c * 128:(c + 1) * 128], interA[:nk, 0:33], True,
                             True, [pkey, 'nsb'], [('psb', 7)])
                    k.ts('dve', rec[:, 0:4], k.ps[7][:, 32:32 + 33 * 3 + 1:33], 1e-30, ALU.max, [('psb', 7)], ['rec'])
                    k.S.op('dve', lambda e: e.reciprocal(out=rec[:, 0:4], in_=rec[:, 0:4]), ['rec'], ['rec'])
                    for c in range(4):
                        tb = 4 * m + c
                        if hi_ == 0:
                            k.ts('dve', impa[:, tb, :], k.ps[7][:, c * 33:c * 33 + 32], rec[:, c:c + 1], ALU.mult,
                                 [('psb', 7), 'rec'], [('impa', tb)])
                        else:
                            k.stt(impa[:, tb, :], k.ps[7][:, c * 33:c * 33 + 32], rec[:, c:c + 1], impa[:, tb, :],
                                  ALU.mult, ALU.add, [('psb', 7), 'rec', ('impa', tb)], [('impa', tb)])
            ik = [('impa', tb) for tb in range(NTB)]
            ia = impa[:].rearrange("p a b -> p (a b)")
            k.tt('dve', ia, ia, mulc[:], ALU.mult, ik + ['mulc'], ik)
            k.tt('dve', ia, ia, addc[:], ALU.add, ik + ['mulc'], ik)
            for tb in range(NTB):
                k.S.op('dve', (lambda o, i_: (lambda e: e.max(out=o, in_=i_)))(top8[:, :], impa[:, tb, :]),
                       [('impa', tb)], ['top8'])
                k.ts('dve', top8[:, 7:8], top8[:, 7:8], -5e29, ALU.max, ['top8'], ['top8'])
                k.ts('dve', nsl[:, tb, :], impa[:, tb, :], top8[:, 7:8], ALU.is_ge, [('impa', tb), 'top8'],
                     [('nsl', tb)])
                k.ts('dve', nsl[:, tb, :], nsl[:, tb, :], -1.0, ALU.add, [('nsl', tb)], [('nsl', tb)], s2=-NEG,
                     op1=ALU.mult)
            for q4 in range(4):
                for u in range(4):
                    tb = 4 * q4 + u
                    k.tr(k.ps[7][:32, u * 128:(u + 1) * 128], nsl[:, tb, :], C['ident_f'][:], [('nsl', tb), 'const'],
                         [('psb', 7)])
                k.cp('dve', nselT[:, q4 * 512:(q4 + 1) * 512], k.ps[7][:32, :], [('psb', 7)], ['nselT'])
            k.S.flush()
            cnt = 0
            for hi_ in range(4):
                h = 4 * g + hi_
                hb = h % 2
                k.dma('sp', qt[hb][:], FMB[h], [], [('qt', hb)])
                k.dma('sp', bt[hb][:], BTH[h], [], [('bt', hb)])
                k.dma('sp', zt[hb][:], ZT[h], [], [('zt', hb)])
                gst, gkey = A.gst.next()
                for m in range(4):
                    gb = cnt % 2
                    cnt += 1
                    k.dma('sp', gat[gb][:], GL[3 * h:3 * h + 3, m * 512:(m + 1) * 512].partition_broadcast(128), [],
                          [('gat', gb)])
                    ac = acc[gb]
                    ackey = ('acc', gb)
                    for br in range(3):
                        (ob, db), _ = A.ob.next()
                        A.zero_init([ob, db])
                        if br == 0:
                            tiles = [cmp_tile(hb, m)]
                        elif br == 1:
                            tiles = []
                            for j in range(0, 4 * m + 4):
                                r = j - 4 * m
                                lo = 0 if r < 0 else 128 * r
                                adds = [(blksel[:32, j * 128:(j + 1) * 128], nselT[:32, m * 512 + lo:(m + 1) * 512],
                                         128, lo, 512, ['nsb', 'nselT'])]
                                if r >= 0:
                                    for hl in range(2):
                                        adds.append((ident[:], bt[hb][:, hl, 0:128], 128, lo, lo + 128,
                                                     ['const', ('bt', hb)]))
                                if -1 <= r < 3:
                                    for hl in range(2):
                                        adds.append((ident[:], bt[hb][:, hl, 128:256], 128, 128 * (r + 1),
                                                     128 * (r + 2), ['const', ('bt', hb)]))
                                tiles.append(dict(kT=kst[:, j * 128:(j + 1) * 128], kkey='kst', nk=128, lo=lo, hi=512,
                                                  adds=adds, v=vst[:, j, :], vkey='vst', j=j))
                        else:
                            tiles = []
                            for j in range(max(0, 4 * m - 4), 4 * m + 4):
                                r = j - 4 * m
                                adds = []
                                if r >= 0:
                                    lo, hi = 128 * r, 512
                                    for hl in range(2):
                                        adds.append((ident[:], bt[hb][:, hl, 0:128], 128, lo, lo + 128,
                                                     ['const', ('bt', hb)]))
                                else:
                                    lo, hi = 0, 128 * (5 + r)
                                    adds.append((ident[:], C['md4'][:], 128, 128 * (4 + r), 128 * (5 + r), ['const']))
                                if -1 <= r < 3:
                                    for hl in range(2):
                                        adds.append((ident[:], bt[hb][:, hl, 128:256], 128, 128 * (r + 1),
                                                     128 * (r + 2), ['const', ('bt', hb)]))
                                tiles.append(dict(kT=kwt[:, j * 128:(j + 1) * 128], kkey='kwt', nk=128, lo=lo, hi=hi,
                                                  adds=adds, v=vwt[:, j, :], vkey='vwt', j=j))
                        softmax_tiles(A, qt[hb], ('qt', hb), m, tiles, ob, db)
                        l2, l2k = A.ft.next()
                        k.act(l2[:, :], k.ps[db][:, :], AF.Ln, [('psb', db), 'const'], [l2k], bias=C['tinycol'][:],
                              scale=1.0)
                        k.act(l2[:, :], l2[:, :], AF.Exp, [l2k], [l2k], scale=-1.0)
                        k.tt('pool', l2[:, :], l2[:, :], gat[gb][:, br, :], ALU.mult, [l2k, ('gat', gb)], [l2k])
                        if br == 0:
                            k.tt('dve', ac[:, :], k.ps[ob][:, :], l2[:, :], ALU.mult, [('psb', ob), l2k], [ackey])
                        else:
                            k.tt('dve', l2[:, :], k.ps[ob][:, :], l2[:, :], ALU.mult, [('psb', ob), l2k], [l2k])
                            k.tt('pool', ac[:, :], ac[:, :], l2[:, :], ALU.add, [ackey, l2k], [ackey])
                    l1, l1k = silu_parts(A, zt[hb], ('zt', hb), m)
                    k.act(l1[:, :], l1[:, :], AF.Exp, [l1k], [l1k], scale=-1.0)
                    k.tt('pool', l1[:, :], l1[:, :], zt[hb][:, m * 512:(m + 1) * 512], ALU.mult, [l1k, ('zt', hb)],
                         [l1k])
                    k.tt('dve', gst[:, m * 512:(m + 1) * 512], ac[:, :], l1[:, :], ALU.mult, [ackey, l1k],
                         [(gkey, m)])
                k.dma('sp', G[h], gst[:], [(gkey, m) for m in range(4)], [])
                k.S.flush()


CF_NAMES = ['ident_f', 'ones_f', 'tri_f', 'sel64_f', 'ustrict_f', 'mstrict_f']
CB_LAYOUT = [('ident_bf', 128), ('ones_bf', 128), ('zeros_bf', 512), ('mcausal', 128), ('mpos_bf', 128), ('md4', 128)]
CS_LAYOUT = [('epscol', 1), ('tinycol', 1), ('rb31', 16), ('sinkb', 16), ('fbb', 256)]


def rel_bucket_np(dist):
    dist = np.maximum(dist, 0)
    ratio = np.maximum(dist, 16).astype(np.float32) / np.float32(16)
    large = 16 + (np.log(ratio) / np.float32(np.log(8.0)) * np.float32(16)).astype(np.int32)
    return np.where(dist < 16, dist, np.minimum(large, 31)).astype(np.int64)


def host_consts():
    p = np.arange(128)
    I = np.eye(128, dtype=np.float32)
    cf = {
        'ident_f': I,
        'ones_f': np.ones((128, 128), np.float32),
        'tri_f': (p[:, None] <= p[None, :]).astype(np.float32),
        'sel64_f': np.zeros((128, 128), np.float32),
        'ustrict_f': (p[:, None] > p[None, :]).astype(np.float32),
        'mstrict_f': (p[:, None] < p[None, :]).astype(np.float32),
    }
    cf['sel64_f'][64, :] = 1.0
    cb = {
        'ident_bf': I,
        'ones_bf': np.ones((128, 128), np.float32),
        'zeros_bf': np.zeros((128, 512), np.float32),
        'mcausal': np.where(p[:, None] > p[None, :], NEG, 0.0).astype(np.float32),
        'mpos_bf': np.where(p[:, None] >= p[None, :], -NEG, 0.0).astype(np.float32),
        'md4': np.where(p[None, :] < p[:, None], 0.0, NEG).astype(np.float32),
    }
    cfa = np.concatenate([cf[n] for n in CF_NAMES], axis=1)
    cba = np.concatenate([cb[n] for n, _ in CB_LAYOUT], axis=1)
    return np.ascontiguousarray(cfa), np.ascontiguousarray(cba)


def bias_gather_indices():
    p = np.arange(128)[:, None]
    f = np.arange(128)[None, :]
    idx = np.zeros((128, NBT), np.int64)
    msk = np.zeros((128, NBT), np.float32)
    d0 = f - p
    idx[:, 0:128] = rel_bucket_np(d0)
    msk[:, 0:128] = np.where(d0 >= 0, 0.0, NEG)
    d1 = 128 + f - p
    idx[:, 128:256] = rel_bucket_np(d1)
    idx[:, 256:384] = rel_bucket_np(d1)
    msk[:, 256:384] = np.where(d1 < 128, 0.0, NEG)
    kk = np.arange(128)[:, None]
    ff = np.arange(512)[None, :]
    dc = ff - 16 * kk + 113
    idx[:, 384:896] = rel_bucket_np(dc)
    msk[:, 384:896] = np.where(dc >= 0, 0.0, NEG)
    msk[40:, 384:896] = 0.0
    return idx, msk


def phase_bias(k, C, btg, bmask, BTH):
    nc = k.nc
    with ExitStack() as ph:
        bm = sbt(nc, ph, "bmk", [128, NBT], F32)
        bg = [sbt(nc, ph, "bg%d" % i, [128, NBT], F32) for i in range(2)]
        bo = [sbt(nc, ph, "bo%d" % i, [128, 2, NBT], BF16) for i in range(2)]
        k.dma('sp', bm[:], bmask, [], ['bmk'])
        for h in range(NH):
            b = h % 2
            k.dma('sp', bg[b][:], btg[h], [], [('bg', b)])
            k.ts('dve', bg[b][:], bg[b][:], C['rb31'][:, h:h + 1], ALU.subtract, [('bg', b), 'const'], [('bg', b)],
                 s2=1.0 / SCALE, op1=ALU.mult)
            k.tt('dve', bg[b][:], bg[b][:], bm[:], ALU.add, [('bg', b), 'bmk'], [('bg', b)])
            k.cp('dve', bo[b][:, 0, :], bg[b][:], [('bg', b)], [('bo', b)])
            k.tt('dve', bo[b][:, 1, :], bg[b][:], bo[b][:, 0, :], ALU.subtract, [('bg', b), ('bo', b)], [('bo', b)])
            k.dma('sp', BTH[h], bo[b][:], [('bo', b)], [])
        k.S.flush()


LAYER_KIND = ['nsa', 'swa', 'sb', 'fox']
WIN_COLS = [7216, 5120, 8192, 8208]


def build_program(layers, dbg=False):
    nc = bass.Bass("TRN2", target_bir_lowering=False)
    dt_in = lambda name, shape: nc.dram_tensor(name, shape, F32, kind="ExternalInput")
    x_in = dt_in("x", [SEQ, DM]).ap()
    x_out = nc.dram_tensor("xo", [SEQ, DM], F32, kind="ExternalOutput").ap()
    cfa = dt_in("cfa", [128, 128 * len(CF_NAMES)]).ap()
    cba = dt_in("cba", [128, sum(n for _, n in CB_LAYOUT)]).ap()
    csa = dt_in("csa", [128, sum(n for _, n in CS_LAYOUT)]).ap()
    LW = {}
    for l in layers:
        LW[l] = dict(w_in=dt_in("w_in%d" % l, [DM, WIN_COLS[l]]).ap(), w_out=dt_in("w_out%d" % l, [DM, DM]).ap(),
                     ln_g=dt_in("ln_g%d" % l, [DM]).ap(), ln_b=dt_in("ln_b%d" % l, [DM]).ap())
    need_bias = any(l in (0, 1) for l in layers)
    if need_bias:
        btg = dt_in("btg", [NH, 128, NBT])
        bmask = dt_in("bmask", [128, NBT]).ap()
    NS = {}
    if 0 in layers:
        NS = nsa_inputs(nc, dt_in)
    sk = "ExternalOutput" if dbg else "Internal"
    D = dict(
        XR=nc.dram_tensor("XR", [SEQ, DM], F32, kind="Internal").ap(),
        FMB=nc.dram_tensor("FMB", [32, 128, SEQ], BF16, kind=sk),
        VS=nc.dram_tensor("VS", [16, 128, NTB, 128], BF16, kind=sk),
        ZT=nc.dram_tensor("ZT", [16, 128, SEQ], F32, kind=sk),
        GL=nc.dram_tensor("GL", [48, SEQ], F32, kind=sk),
        FL=nc.dram_tensor("FL", [128, NTB, 16], F32, kind=sk).ap(),
        G=nc.dram_tensor("G", [16, 128, SEQ], BF16, kind=sk),
        BTH=nc.dram_tensor("BTH", [16, 128, 2, NBT], BF16, kind=sk),
    )
    with ExitStack() as es:
        k = K(nc, es)
        cf_sb = sbt(nc, es, "cf_sb", [128, 128 * len(CF_NAMES)], F32)
        cb_sb = sbt(nc, es, "cb_sb", [128, sum(n for _, n in CB_LAYOUT)], BF16)
        cs_sb = sbt(nc, es, "cs_sb", [128, sum(n for _, n in CS_LAYOUT)], F32)
        xT = sbt(nc, es, "xT", [128, 16, SEQ], BF16)
        C = {'xT': xT}
        for i, n in enumerate(CF_NAMES):
            C[n] = cf_sb[:, i * 128:(i + 1) * 128]
        o = 0
        for n, w in CB_LAYOUT:
            C[n] = cb_sb[:, o:o + w]
            o += w
        o = 0
        for n, w in CS_LAYOUT:
            C[n] = cs_sb[:, o:o + w]
            o += w
        k.dma('sp', cf_sb[:], cfa, [], ['const'])
        k.dma('pool', cb_sb[:], cba, [], ['const'])
        k.dma('sp', cs_sb[:], csa, [], ['const'])
        k.S.flush()
        if need_bias:
            phase_bias(k, C, btg, bmask, D['BTH'])
        phase_x0(k, C, x_in, D['XR'])
        for li, l in enumerate(layers):
            W = LW[l]
            kind = LAYER_KIND[l]
            FMB, VS, ZT = D['FMB'], D['VS'], D['ZT']
            groups = []
            if kind == 'nsa':
                for i in range(4):
                    groups.append(('FMB', 512 * i, 512, [FMB[4 * i + j] for j in range(4)]))
                for i, base in enumerate([16, 20, 24]):
                    groups.append(('FMB', 2048 + 512 * i, 512, [FMB[base + j] for j in range(4)]))
                groups.append(('TMV', 3584, 512, [VS[j] for j in range(4)]))
                groups.append(('FMB', 4096, 512, [FMB[28 + j] for j in range(4)]))
                groups.append(('TMV', 4608, 512, [VS[4 + j] for j in range(4)]))
                groups.append(('FMF', 5120, 48, [D['GL'][0:48]]))
                for i in range(4):
                    groups.append(('FMF', 5168 + 512 * i, 512, [ZT[4 * i + j] for j in range(4)]))
            elif kind == 'swa':
                for i in range(4):
                    groups.append(('FMB', 512 * i, 512, [FMB[4 * i + j] for j in range(4)]))
                groups.append(('FMB', 2048, 512, [FMB[16 + j] for j in range(4)]))
                groups.append(('TMV', 2560, 512, [VS[j] for j in range(4)]))
                for i in range(4):
                    groups.append(('FMF', 3072 + 512 * i, 512, [ZT[4 * i + j] for j in range(4)]))
            else:
                for i in range(8):
                    groups.append(('FMB', 512 * i, 512, [FMB[4 * i + j] for j in range(4)]))
                for i in range(4):
                    groups.append(('TMV', 4096 + 512 * i, 512, [VS[4 * i + j] for j in range(4)]))
                zc = 6144
                if kind == 'fox':
                    groups.append(('TMF', 6144, 16, D['FL']))
                    zc = 6160
                for i in range(4):
                    groups.append(('FMF', zc + 512 * i, 512, [ZT[4 * i + j] for j in range(4)]))
            phase_P(k, C, W['w_in'], groups)
            if kind == 'nsa':
                phase_A_nsa(k, C, D, NS)
            elif kind == 'swa':
                phase_A_swa(k, C, D)
            elif kind == 'sb':
                phase_A_sb(k, C, D)
            else:
                phase_A_fox(k, C, D)
            last = (li == len(layers) - 1)
            phase_B(k, C, W['w_out'], W['ln_g'].partition_broadcast(128), W['ln_b'].partition_broadcast(128),
                    D['G'], D['XR'], x_out, last)
    print("program instructions:", k.S.ninst, flush=True)
    return nc


SUF = 'abcd'


def layer_inputs(inputs, layers):
    cfa, cba = host_consts()
    rb = np.asarray(inputs['rel_bias'], np.float32)
    csa = np.zeros((128, sum(n for _, n in CS_LAYOUT)), np.float32)
    csa[:, 0] = LN_EPS
    csa[:, 1] = 1e-18
    csa[:, 2:18] = rb[31][None, :]
    csa[:, 18:34] = np.asarray(inputs['sinks_b'], np.float32)[None, :]
    csa[:, 34:290] = np.tile(np.asarray(inputs['fgate_bias_d'], np.float32), NTB)[None, :]
    m = {'cfa': cfa, 'cba': cba, 'csa': csa}
    names = [('w_in_a', 'w_out_a', 'ln_g_a', 'ln_b_a'), ('w_in_b', 'w_out_b', 'ln_g_b', 'ln_b_b'),
             ('w_in_c', 'w_out_c', 'ln_g_c', 'ln_b_c'), ('w_in_d', 'w_out_d', 'ln_g_d', 'ln_b_d')]
    for l in layers:
        wi, wo, lg, lb = names[l]
        m['w_in%d' % l] = np.ascontiguousarray(inputs[wi], np.float32)
        m['w_out%d' % l] = np.ascontiguousarray(inputs[wo], np.float32)
        m['ln_g%d' % l] = np.ascontiguousarray(inputs[lg], np.float32)
        m['ln_b%d' % l] = np.ascontiguousarray(inputs[lb], np.float32)
    if any(l in (0, 1) for l in layers):
        idx, msk = bias_gather_indices()
        m['btg'] = np.ascontiguousarray(np.transpose(rb[idx], (2, 0, 1)))
        m['bmask'] = msk
    if 0 in layers:
        m.update(nsa_host_inputs(inputs))
    return m


_PROG_CACHE = {}


def run_layers(x, inputs, layers, dbg=False, ncores=4):
    key = (tuple(layers), dbg)
    if key not in _PROG_CACHE:
        _PROG_CACHE[key] = build_program(layers, dbg)
    nc = _PROG_CACHE[key]
    shared = layer_inputs(inputs, layers)
    in_maps = []
    for b in range(ncores):
        mm_ = dict(shared)
        mm_['x'] = np.ascontiguousarray(x[b], np.float32)
        in_maps.append(mm_)
    res = run_bass_kernel_spmd(nc, in_maps, core_ids=list(range(ncores)), trace=bool(int(os.environ.get('DBG_TRACE', '0'))))
    if res.exec_time_ns is not None:
        print('exec_time_ns', res.exec_time_ns, flush=True)
    return res


FUSED = True


def kernel(**inputs):
    x = np.asarray(inputs['x'], np.float32)
    if FUSED:
        res = run_layers(x, inputs, [0, 1, 2, 3])
        return np.stack([res.results[b]['xo'] for b in range(4)], axis=0).astype(np.float32)
    for l in range(4):
        res = run_layers(x, inputs, [l])
        x = np.stack([res.results[b]['xo'] for b in range(4)], axis=0).astype(np.float32)
    return x
```
